# Optimizing a Trainium2 kernel written in Bass

```python
import jax, jax.numpy as jnp
from jax import lax
import numpy as np

D_MODEL = 1024
BATCH = 8
SEQ = 2048
DEPTH = 2
DEC_BATCH = 128
DEC_SEQ = 1
PAST_LEN = 16384
PAGE_SIZE = 128

GM_HEADS = 4
GM_HEAD_DIM = D_MODEL // 8
GM_WIDTH = GM_HEADS * GM_HEAD_DIM
GM_CHUNK = 128
ML_HEADS = 4
ML_HEAD_DIM = D_MODEL // 8
ML_WIDTH = ML_HEADS * ML_HEAD_DIM
ML_CONV = 4
ML_CHUNK = 64
D_MIX = GM_WIDTH + ML_WIDTH
N_IN = 2 * GM_WIDTH + 2 * ML_WIDTH + 2 * ML_HEADS
MEM_TOKENS = 256
XA_HEADS = 4
XA_HEAD_DIM = D_MODEL // XA_HEADS
D_FF = 4 * D_MODEL
EPS = 1e-6
NEG = -1e30

kernel_name = "hybrid_gmlp_mlstm_memxattn_step"


def _rmsnorm(x, g):
    xf = x.astype(jnp.float32)
    y = xf * lax.rsqrt(jnp.mean(xf * xf, axis=-1, keepdims=True) + EPS)
    return (y * g.astype(jnp.float32)).astype(x.dtype)


def _chunk_gmlp(z, v_norm_g, ws, bs):
    z = jax.nn.gelu(z)
    u, v = jnp.split(z, 2, axis=-1)
    v = _rmsnorm(v, v_norm_g)
    B, S, _ = v.shape
    n_chunks = -(-S // GM_CHUNK)
    pad = n_chunks * GM_CHUNK - S
    vp = jnp.pad(v, ((0, 0), (0, pad), (0, 0))).reshape(B, n_chunks, GM_CHUNK, GM_HEADS, GM_HEAD_DIM)
    causal = jnp.tril(jnp.ones((GM_CHUNK, GM_CHUNK), dtype=bool))
    w = jnp.where(causal[None], ws, 0).astype(vp.dtype)
    mixed = jnp.einsum('hts,bnshd->bnthd', w, vp) + bs.T[None, None, :, :, None]
    gate = mixed.reshape(B, n_chunks * GM_CHUNK, GM_WIDTH)[:, :S]
    return u * gate, v


def _causal_conv(x_ext, w, b, S):
    out = b
    for j in range(ML_CONV):
        out = out + x_ext[:, j:j + S] * w[j]
    return out


def _mlstm(q, k, v, ig, lf, C0, n0, m0):
    B, S, H, D = q.shape
    L = min(ML_CHUNK, S)
    nc = -(-S // L)
    pad = nc * L - S
    valid = (jnp.arange(nc * L) < S)[None, :, None]
    padf = lambda a: jnp.pad(a, [(0, 0), (0, pad)] + [(0, 0)] * (a.ndim - 2))
    q, k, v = padf(q), padf(k), padf(v)
    ig = jnp.where(valid, padf(ig), NEG)
    lf = jnp.where(valid, padf(lf), 0.0)
    to_chunks = lambda a: jnp.moveaxis(a.reshape((B, nc, L) + a.shape[2:]), 1, 0)
    causal = jnp.tril(jnp.ones((L, L), dtype=bool))[None, :, :, None]

    def step(carry, inp):
        C, n, m = carry
        qc, kc, vc, igc, lfc = inp
        b = jnp.cumsum(lfc, axis=1)
        a = b + m[:, None, :]
        dlog = jnp.where(causal, b[:, :, None, :] - b[:, None, :, :] + igc[:, None, :, :], NEG)
        mt = jnp.maximum(a, jnp.max(dlog, axis=2))
        w_inter = jnp.exp(a - mt)
        s = jnp.einsum('bthd,bshd->btsh', qc, kc) * jnp.exp(dlog - mt[:, :, None, :])
        num = jnp.einsum('btsh,bshd->bthd', s, vc) + w_inter[..., None] * jnp.einsum('bthk,bhkv->bthv', qc, C)
        den = jnp.sum(s, axis=2) + w_inter * jnp.einsum('bthk,bhk->bth', qc, n)
        h = num / jnp.maximum(jnp.abs(den), jnp.exp(-mt))[..., None]
        bL = b[:, -1]
        aL = bL + m
        wlog = bL[:, None, :] - b + igc
        m_new = jnp.maximum(aL, jnp.max(wlog, axis=1))
        w_s = jnp.exp(wlog - m_new[:, None, :])
        decay = jnp.exp(aL - m_new)
        C_new = decay[..., None, None] * C + jnp.einsum('bsh,bshk,bshv->bhkv', w_s, kc, vc)
        n_new = decay[..., None] * n + jnp.einsum('bsh,bshk->bhk', w_s, kc)
        return (C_new, n_new, m_new), h

    (C, n, m), h = lax.scan(step, (C0, n0, m0), tuple(map(to_chunks, (q, k, v, ig, lf))))
    h = jnp.moveaxis(h, 0, 1).reshape(B, nc * L, H, D)[:, :S]
    return h, C, n, m


def _mem_kv(mem, g_mem, w_ck, w_cv):
    B = mem.shape[0]
    mn = _rmsnorm(mem, g_mem)
    mk = (mn @ w_ck).reshape(B, MEM_TOKENS, XA_HEADS, XA_HEAD_DIM)
    mv = (mn @ w_cv).reshape(B, MEM_TOKENS, XA_HEADS, XA_HEAD_DIM)
    return mk, mv


def _layer(x, mem_k, mem_v, conv_buf, C0, n0, m0, lw):
    (g_mix, w_in, gm_v_g, gm_ws, gm_bs, ml_conv_w, ml_conv_b, ml_wq, ml_wk, ml_wv,
     ml_b_i, ml_b_f, ml_out_g, ml_skip, w_out, g_xa, w_cq, w_co, g_ffn, w_up, w_down) = lw
    f32 = jnp.float32
    B, S, _ = x.shape
    h = _rmsnorm(x, g_mix)
    proj = h @ w_in
    c0 = 2 * GM_WIDTH
    c1 = c0 + ML_WIDTH
    c2 = c1 + ML_WIDTH
    c3 = c2 + ML_HEADS
    z_gm, xm, o_pre, ig_pre, fg_pre = jnp.split(proj, [c0, c1, c2, c3], axis=-1)
    y_gm, v_gm = _chunk_gmlp(z_gm, gm_v_g, gm_ws, gm_bs)
    x_ext = jnp.concatenate([conv_buf.astype(xm.dtype), xm], axis=1)
    conv_act = jax.nn.silu(_causal_conv(x_ext, ml_conv_w, ml_conv_b, S))
    heads = lambda a: a.reshape(B, S, ML_HEADS, ML_HEAD_DIM)
    q = jnp.einsum('bshd,hde->bshe', heads(conv_act), ml_wq)
    k = jnp.einsum('bshd,hde->bshe', heads(conv_act), ml_wk) * (ML_HEAD_DIM ** -0.5)
    v = jnp.einsum('bshd,hde->bshe', heads(xm), ml_wv)
    ig = (ig_pre + ml_b_i).astype(f32)
    lf = jax.nn.log_sigmoid((fg_pre + ml_b_f).astype(f32))
    hc, C, n, m = _mlstm(q.astype(f32), k.astype(f32), v.astype(f32), ig, lf,
                         C0.astype(f32), n0.astype(f32), m0.astype(f32))
    hc = _rmsnorm(hc, ml_out_g.reshape(ML_HEADS, ML_HEAD_DIM)).reshape(B, S, ML_WIDTH).astype(x.dtype)
    y_ml = jax.nn.sigmoid(o_pre) * (hc + ml_skip * conv_act)
    x = x + jnp.concatenate([y_gm, y_ml], axis=-1) @ w_out
    hq = (_rmsnorm(x, g_xa) @ w_cq).reshape(B, S, XA_HEADS, XA_HEAD_DIM)
    sc = jnp.einsum('bshd,bmhd->bhsm', hq, mem_k.astype(hq.dtype)).astype(f32) * (XA_HEAD_DIM ** -0.5)
    p = jax.nn.softmax(sc, axis=-1).astype(x.dtype)
    att = jnp.einsum('bhsm,bmhd->bshd', p, mem_v.astype(x.dtype)).reshape(B, S, D_MODEL)
    x = x + att @ w_co
    hf = _rmsnorm(x, g_ffn)
    x = x + jnp.square(jax.nn.relu(hf @ w_up)) @ w_down
    new_buf = x_ext[:, -(ML_CONV - 1):]
    return x, v_gm, new_buf, C, n, m


def setup_inputs(seed: int = 0) -> dict:
    key = jax.random.key(seed)
    ks = iter(jax.random.split(key, 48))
    f32 = jnp.float32
    nrm = lambda shape, s: s * jax.random.normal(next(ks), shape, f32)
    gain = lambda shape: 1.0 + nrm(shape, 0.02)
    H, Dh = ML_HEADS, ML_HEAD_DIM
    return {
        "x_prompt": nrm((BATCH, SEQ, D_MODEL), 1.0),
        "x_sample": nrm((DEC_BATCH, DEC_SEQ, D_MODEL), 1.0),
        "mem_prompt": nrm((BATCH, MEM_TOKENS, D_MODEL), 1.0),
        "cache_mem_k": nrm((DEPTH, DEC_BATCH, MEM_TOKENS, XA_HEADS, XA_HEAD_DIM), 1.0),
        "cache_mem_v": nrm((DEPTH, DEC_BATCH, MEM_TOKENS, XA_HEADS, XA_HEAD_DIM), 1.0),
        "state_C": nrm((DEPTH, DEC_BATCH, H, Dh, Dh), 0.1),
        "state_n": nrm((DEPTH, DEC_BATCH, H, Dh), 0.1),
        "state_m": 2.0 + nrm((DEPTH, DEC_BATCH, H), 0.5),
        "state_conv": nrm((DEPTH, DEC_BATCH, ML_CONV - 1, ML_WIDTH), 1.0),
        "norm_mix_g": gain((DEPTH, D_MODEL)),
        "w_in": nrm((DEPTH, D_MODEL, N_IN), D_MODEL ** -0.5),
        "gm_v_norm_g": gain((DEPTH, GM_WIDTH)),
        "gm_ws": nrm((DEPTH, GM_HEADS, GM_CHUNK, GM_CHUNK), GM_CHUNK ** -0.5),
        "gm_bs": 1.0 + nrm((DEPTH, GM_HEADS, GM_CHUNK), 0.1),
        "ml_conv_w": nrm((DEPTH, ML_CONV, ML_WIDTH), 0.5),
        "ml_conv_b": nrm((DEPTH, ML_WIDTH), 0.01),
        "ml_wq": nrm((DEPTH, H, Dh, Dh), Dh ** -0.5),
        "ml_wk": nrm((DEPTH, H, Dh, Dh), Dh ** -0.5),
        "ml_wv": nrm((DEPTH, H, Dh, Dh), Dh ** -0.5),
        "ml_b_i": nrm((DEPTH, H), 0.1),
        "ml_b_f": jnp.linspace(3.0, 6.0, H, dtype=f32)[None, :] + nrm((DEPTH, H), 0.1),
        "ml_out_norm_g": gain((DEPTH, ML_WIDTH)),
        "ml_skip": 1.0 + nrm((DEPTH, ML_WIDTH), 0.1),
        "w_out": nrm((DEPTH, D_MIX, D_MODEL), D_MIX ** -0.5),
        "norm_mem_g": gain((DEPTH, D_MODEL)),
        "w_ck": nrm((DEPTH, D_MODEL, D_MODEL), D_MODEL ** -0.5),
        "w_cv": nrm((DEPTH, D_MODEL, D_MODEL), D_MODEL ** -0.5),
        "norm_xa_g": gain((DEPTH, D_MODEL)),
        "w_cq": nrm((DEPTH, D_MODEL, D_MODEL), D_MODEL ** -0.5),
        "w_co": nrm((DEPTH, D_MODEL, D_MODEL), D_MODEL ** -0.5),
        "norm_ffn_g": gain((DEPTH, D_MODEL)),
        "w_up": nrm((DEPTH, D_MODEL, D_FF), D_MODEL ** -0.5),
        "w_down": nrm((DEPTH, D_FF, D_MODEL), D_FF ** -0.5),
        "norm_f_g": gain((D_MODEL,)),
    }


def reference(x_prompt, x_sample, mem_prompt, cache_mem_k, cache_mem_v, state_C, state_n, state_m, state_conv,
              norm_mix_g, w_in, gm_v_norm_g, gm_ws, gm_bs, ml_conv_w, ml_conv_b, ml_wq, ml_wk, ml_wv,
              ml_b_i, ml_b_f, ml_out_norm_g, ml_skip, w_out, norm_mem_g, w_ck, w_cv, norm_xa_g, w_cq, w_co,
              norm_ffn_g, w_up, w_down, norm_f_g):
    f32 = jnp.float32
    B = x_prompt.shape[0]
    xp, xs = x_prompt, x_sample
    mk_p, mv_p, C_p, n_p, m_p, cv_p = [], [], [], [], [], []
    C_s, n_s, m_s, cv_s, gv_s = [], [], [], [], []
    for l in range(DEPTH):
        lw = (norm_mix_g[l], w_in[l], gm_v_norm_g[l], gm_ws[l], gm_bs[l], ml_conv_w[l], ml_conv_b[l],
              ml_wq[l], ml_wk[l], ml_wv[l], ml_b_i[l], ml_b_f[l], ml_out_norm_g[l], ml_skip[l], w_out[l],
              norm_xa_g[l], w_cq[l], w_co[l], norm_ffn_g[l], w_up[l], w_down[l])
        mk, mv = _mem_kv(mem_prompt, norm_mem_g[l], w_ck[l], w_cv[l])
        xp, _, bp, Cp, np_, mp = _layer(
            xp, mk, mv,
            jnp.zeros((B, ML_CONV - 1, ML_WIDTH), xp.dtype),
            jnp.zeros((B, ML_HEADS, ML_HEAD_DIM, ML_HEAD_DIM), f32),
            jnp.zeros((B, ML_HEADS, ML_HEAD_DIM), f32),
            jnp.zeros((B, ML_HEADS), f32), lw)
        mk_p.append(mk); mv_p.append(mv); C_p.append(Cp); n_p.append(np_); m_p.append(mp); cv_p.append(bp)
        xs, vs, bsm, Cs, ns, ms = _layer(
            xs, cache_mem_k[l], cache_mem_v[l], state_conv[l], state_C[l], state_n[l], state_m[l], lw)
        C_s.append(Cs); n_s.append(ns); m_s.append(ms); cv_s.append(bsm); gv_s.append(vs)
    y_prompt = _rmsnorm(xp, norm_f_g)
    y_sample = _rmsnorm(xs, norm_f_g)
    st = lambda a: jnp.stack(a, axis=0)
    return (y_prompt, y_sample, st(mk_p), st(mv_p), st(C_p), st(n_p), st(m_p), st(cv_p),
            st(C_s), st(n_s), st(m_s), st(cv_s), st(gv_s))
```

```python
import math
from contextlib import ExitStack

import numpy as np
import concourse.bass as bass
import concourse.mybir as mybir
from concourse.bass_utils import run_bass_kernel_spmd

F32 = mybir.dt.float32
BF16 = mybir.dt.bfloat16
FP8 = mybir.dt.float8e4
AF = mybir.ActivationFunctionType
ALU = mybir.AluOpType
AX = mybir.AxisListType

NCORES = 8
DEPTH = 2
D = 1024
NP = 2048
NS = 16
NT = NP // 128
MEM = 256
NIN = 2056
DFF = 4096
EPS = 1e-6
ENGS = ("pe", "act", "dve", "pool", "sp")


class _Op:
    __slots__ = ("eng", "fn", "deps", "dma", "dsem", "dval", "inc", "cnt", "prev_dma", "gen", "persist")

    def __init__(self, eng, fn, dma):
        self.eng = eng
        self.fn = fn
        self.deps = set()
        self.dma = dma
        self.inc = False
        self.cnt = 0
        self.dsem = None
        self.dval = 0
        self.prev_dma = None
        self.gen = 0
        self.persist = False


class Sched:
    def __init__(self, nc, stack, n_dma_sems=24):
        self.nc = nc
        self.sem = {e: stack.enter_context(nc.semaphore("s_" + e)) for e in ENGS}
        self.dsems = [stack.enter_context(nc.semaphore("d_%d" % i)) for i in range(n_dma_sems)]
        self.dcount = [0] * n_dma_sems
        self.dlast = [None] * n_dma_sems
        self.dpool = {"pool": list(range(0, 8)), "sp": list(range(8, n_dma_sems))}
        self.drr = {"pool": 0, "sp": 0}
        self.count = {e: 0 for e in ENGS}
        self.pending = {e: [] for e in ENGS}
        self.writer = {}
        self.readers = {}
        self.known = {e: {} for e in ENGS}
        self.n_instr = 0
        self.gen = 0

    def add(self, eng, fn, reads=(), writes=(), dma=False, persist=False):
        op = _Op(eng, fn, dma)
        op.gen = self.gen
        op.persist = persist
        deps = op.deps
        for k in reads:
            w = self.writer.get(k)
            if w is not None:
                deps.add(w)
        for k in writes:
            w = self.writer.get(k)
            if w is not None:
                deps.add(w)
            rl = self.readers.get(k)
            if rl:
                deps.update(rl)
        deps.discard(op)
        if eng == "pe":
            op.deps = deps = {d for d in deps if d.eng != "pe" or d.dma}
        for k in reads:
            self.readers.setdefault(k, []).append(op)
        for k in writes:
            self.writer[k] = op
            self.readers[k] = []
        if dma:
            pl = self.dpool[eng]
            i = pl[self.drr[eng] % len(pl)]
            self.drr[eng] += 1
            op.dsem = i
            self.dcount[i] += 1
            op.dval = 16 * self.dcount[i]
            op.prev_dma = self.dlast[i]
            self.dlast[i] = op
        self.pending[eng].append(op)
        return op

    def flush(self, final=False):
        nc = self.nc
        gen = self.gen
        for e in ENGS:
            for op in self.pending[e]:
                for d in op.deps:
                    if not d.dma and d.gen == gen:
                        d.inc = True
        prev_final = dict(getattr(self, "final_cnt", {}))
        for e in ENGS:
            comp = [op for op in self.pending[e] if not op.dma]
            if comp:
                comp[-1].inc = True
        for e in ENGS:
            c = self.count[e]
            for op in self.pending[e]:
                if op.inc and not op.dma:
                    c += 1
                    op.cnt = c
            self.count[e] = c
        self.final_cnt = dict(self.count)
        pend = self.pending
        self.pending = {e: [] for e in ENGS}
        sched = self
        self.gen += 1

        def emit_engine(e, eng):
            known = sched.known[e]
            for e2, v in prev_final.items():
                if v > 0 and known.get(("e", e2), 0) < v:
                    known[("e", e2)] = v
                    eng.wait_ge(sched.sem[e2], v)
            for op in pend[e]:
                need = {}
                for d in op.deps:
                    if d.dma:
                        s, v = ("d", d.dsem), d.dval
                    elif d.gen != gen:
                        continue
                    else:
                        s, v = ("e", d.eng), d.cnt
                    if need.get(s, 0) < v:
                        need[s] = v
                if op.dma and op.prev_dma is not None:
                    s, v = ("d", op.dsem), op.prev_dma.dval
                    if need.get(s, 0) < v:
                        need[s] = v
                for s, v in need.items():
                    if known.get(s, 0) >= v:
                        continue
                    known[s] = v
                    semh = sched.dsems[s[1]] if s[0] == "d" else sched.sem[s[1]]
                    eng.wait_ge(semh, v)
                    sched.n_instr += 1
                ins = op.fn(eng)
                sched.n_instr += 1
                if op.dma:
                    ins.then_inc(sched.dsems[op.dsem], 16)
                elif op.inc:
                    ins.then_inc(sched.sem[e], 1)
            if e == "sp":
                for i, last in enumerate(sched.dlast):
                    if last is None or known.get(("d", i), 0) >= last.dval:
                        continue
                    if final or not last.persist:
                        known[("d", i)] = last.dval
                        eng.wait_ge(sched.dsems[i], last.dval)

        with nc.Block() as block:
            @block.tensor
            def _(eng):
                emit_engine("pe", eng)

            @block.scalar
            def _(eng):
                emit_engine("act", eng)

            @block.vector
            def _(eng):
                emit_engine("dve", eng)

            @block.gpsimd
            def _(eng):
                emit_engine("pool", eng)

            @block.sync
            def _(eng):
                emit_engine("sp", eng)


IN_SPECS = [
    ("xp", [NP, D]), ("xs", [NS, D]), ("mem", [MEM, D]),
    ("ck", [DEPTH, NS, MEM, D]), ("cv", [DEPTH, NS, MEM, D]),
    ("sC", [DEPTH, NS, 4, 128, 128]), ("sn", [DEPTH, NS, 512]), ("sm", [DEPTH, NS, 4]),
    ("scv", [DEPTH, NS, 3, 512]),
    ("norm_mix_g", [DEPTH, D]), ("w_in", [DEPTH, D, NIN]), ("gm_v_norm_g", [DEPTH, 512]),
    ("gm_ws", [DEPTH, 4, 128, 128]), ("gm_bs", [DEPTH, 4, 128]), ("ml_conv_w", [DEPTH, 4, 512]),
    ("ml_conv_b", [DEPTH, 512]), ("ml_wq", [DEPTH, 4, 128, 128]), ("ml_wk", [DEPTH, 4, 128, 128]),
    ("ml_wv", [DEPTH, 4, 128, 128]), ("ml_b_i", [DEPTH, 4]), ("ml_b_f", [DEPTH, 4]),
    ("ml_out_norm_g", [DEPTH, 512]), ("ml_skip", [DEPTH, 512]), ("w_out", [DEPTH, D, D]),
    ("norm_mem_g", [DEPTH, D]), ("w_ck", [DEPTH, D, D]), ("w_cv", [DEPTH, D, D]),
    ("norm_xa_g", [DEPTH, D]), ("w_cq", [DEPTH, D, D]), ("w_co", [DEPTH, D, D]),
    ("norm_ffn_g", [DEPTH, D]), ("w_up", [DEPTH, D, DFF]), ("w_down", [DEPTH, DFF, D]),
    ("norm_f_g", [D]),
]
OUT_SPECS = [
    ("y_p", [NP, D]), ("y_s", [NS, D]), ("mk_o", [DEPTH, MEM, D]), ("mv_o", [DEPTH, MEM, D]),
    ("C_p", [DEPTH, 4, 128, 128]), ("n_p", [DEPTH, 4, 128]), ("m_p", [DEPTH, 4]), ("cv_p", [DEPTH, 3, 512]),
    ("C_s", [DEPTH, NS, 4, 128, 128]), ("n_s", [DEPTH, NS, 512]), ("m_s", [DEPTH, NS, 4]),
    ("cv_s", [DEPTH, NS, 3, 512]), ("gv_s", [DEPTH, NS, 512]),
]


def build_nc(stages=("mix", "mixs", "attn", "attns", "ffn")):
    nc = bass.Bass("TRN2", target_bir_lowering=False)
    I = {n: nc.dram_tensor(n, s, F32, kind="ExternalInput").ap() for n, s in IN_SPECS}
    O = {n: nc.dram_tensor(n, s, F32, kind="ExternalOutput").ap() for n, s in OUT_SPECS}

    with ExitStack() as top:
        S = Sched(nc, top)

        uniq = [0]

        def sbt(st, name, shape, dt):
            uniq[0] += 1
            return st.enter_context(nc.sbuf_tensor("%s_%d" % (name, uniq[0]), shape, dt))

        class Ctx:
            pass
        c = Ctx()
        c.sink = None
        c._unit = None

        def ADD(eng, fn, r=(), w=(), **kw):
            if c.sink is None:
                S.add(eng, fn, r, w, **kw)
            elif c._unit is not None:
                c._unit.append((eng, fn, tuple(r), tuple(w), kw))
            else:
                c.sink.append([(eng, fn, tuple(r), tuple(w), kw)])

        class _Unit:
            def __enter__(self):
                if c.sink is not None:
                    assert c._unit is None
                    c._unit = []

            def __exit__(self, *a):
                if c.sink is not None:
                    c.sink.append(c._unit)
                    c._unit = None

        def replay(units):
            for u in units:
                for (eng, fn, r, w, kw) in u:
                    S.add(eng, fn, r, w, **kw)

        def merge_units(*streams, gates=()):
            pos = [0] * len(streams)
            out = []
            while True:
                live = [i for i in range(len(streams)) if pos[i] < len(streams[i])]
                for (i, istart, j, n) in gates:
                    if i in live and pos[i] >= istart and pos[j] < min(n, len(streams[j])):
                        live.remove(i)
                if not live:
                    break
                i = min(live, key=lambda i: (pos[i] / float(len(streams[i])), i))
                out.append(streams[i][pos[i]])
                pos[i] += 1
            assert all(pos[i] == len(streams[i]) for i in range(len(streams))), "merge deadlock"
            return out

        c.ADD, c.unit, c.replay, c.merge_units = ADD, _Unit, replay, merge_units
        _js = [0]

        def jslot():
            _js[0] ^= 1
            return _js[0]
        c.jslot = jslot
        _js_cur = lambda: _js[0]
        c._js_cur = _js_cur

        def MM(out, lhsT, rhs, start, stop, r, w):
            ADD("pe", lambda e: e.matmul(out, lhsT=lhsT, rhs=rhs, start=start, stop=stop), r, w)

        def TR(out, in_, ident, r, w):
            ADD("pe", lambda e: e.transpose(out, in_, ident), r, w)

        def ACT(out, in_, func, r, w, **kw):
            ADD("act", lambda e: e.activation(out=out, in_=in_, func=func, **kw), r, w)

        def VE(meth, r, w, eng="dve", **kw):
            ADD(eng, lambda e: getattr(e, meth)(**kw), r, w)

        def DMA(out, in_, r, w, eng="sp", persist=False, **kw):
            ADD(eng, lambda e: e.dma_start(out=out, in_=in_, **kw), r, w, dma=True, persist=persist)

        xP = sbt(top, "xP", [128, NT, D], F32)
        xS = sbt(top, "xS", [NS, D], F32)
        identf = sbt(top, "identf", [128, 128], F32)
        identb = sbt(top, "identb", [128, 128], BF16)
        maskst = sbt(top, "maskst", [128, 128], F32)
        onesf = sbt(top, "onesf", [128, 128], F32)
        onesb = sbt(top, "onesb", [1, 128], BF16)
        sel4 = sbt(top, "sel4", [4, 4, 128], F32)
        selb = sbt(top, "selb", [128, 16, 16], F32)
        gbc = sbt(top, "gbc", [128, D], F32)
        junk = sbt(top, "junk", [128, 2, D], FP8)
        hn = sbt(top, "hn", [128, 2, D], BF16)
        nst = sbt(top, "nst", [128, 8, 2], F32)
        PS = top.enter_context(nc.psum_tensor("PS", [128, 7, 512], F32))
        PT = top.enter_context(nc.psum_tensor("PT", [128, 2, 4, 128], BF16))

        S.add("pool", lambda e: e.memset(identf[:], 0.0), (), ["identf"])
        S.add("pool", lambda e: e.affine_select(out=identf[:], in_=identf[:], pattern=[[-1, 128]],
                                                compare_op=ALU.not_equal, fill=1.0, base=0, channel_multiplier=1),
              ["identf"], ["identf"])
        VE("tensor_copy", ["identf"], ["identb"], out=identb[:], in_=identf[:])
        S.add("pool", lambda e: e.memset(onesf[:], 1.0), (), ["onesf"])
        S.add("pool", lambda e: e.memset(onesb[:], 1.0), (), ["onesb"])
        S.add("pool", lambda e: e.affine_select(out=maskst[:], in_=onesf[:], pattern=[[1, 128]],
                                                compare_op=ALU.is_ge, fill=0.0, base=0, channel_multiplier=-1),
              ["onesf"], ["maskst"])
        S.add("pool", lambda e: e.affine_select(out=sel4[:], in_=onesf[0:4, 0:128].unsqueeze(1).to_broadcast([4, 4, 128]),
                                                pattern=[[1, 4], [0, 128]],
                                                compare_op=ALU.is_equal, fill=0.0, base=0, channel_multiplier=-1),
              ["onesf"], ["sel4"])
        S.add("pool", lambda e: e.affine_select(out=selb[:], in_=onesf[:, 0:16].unsqueeze(1).to_broadcast([128, 16, 16]),
                                                pattern=[[1, 16], [-1, 16]],
                                                compare_op=ALU.is_equal, fill=0.0, base=0, channel_multiplier=0),
              ["onesf"], ["selb"])

        for g in range(4):
            DMA(xP[:, 4 * g:4 * g + 4, :], I["xp"][512 * g:512 * (g + 1), :].rearrange("(t p) d -> p t d", p=128),
                [], [("xP", 4 * g + j) for j in range(4)])
        DMA(xS[:], I["xs"], [], ["xS"])

        rot_i = [0]

        def rot():
            i = rot_i[0] % 2
            rot_i[0] += 1
            return PS[:, i, :], ("PS", i)

        nrm_i = [0]

        def make_sel16(st, dt=F32):
            t = sbt(st, "sel16", [16, 16, 128], dt)
            S.add("pool", lambda e: e.affine_select(out=t[:], in_=onesf[0:16, 0:128].unsqueeze(1).to_broadcast([16, 16, 128]),
                                                    pattern=[[1, 16], [0, 128]],
                                                    compare_op=ALU.is_equal, fill=0.0, base=0, channel_multiplier=-1),
                  ["onesf"], ["sel16"])
            return t

        def load_gain(src):
            DMA(gbc[:], src.partition_broadcast(128), [], ["gbc"])

        def _norm_p1(xin, R, rk):
            i = nrm_i[0]
            nrm_i[0] += 1
            b4 = i % 8
            ss = nst[0:R, b4, 0:1]
            rs_ = nst[0:R, b4, 1:2]
            ACT(junk[0:R, jslot(), :], xin, AF.Square, rk, [("junk", _js_cur()), ("nst", b4)], accum_out=ss, saturate=False)
            rstd(rs_, ss, 1.0 / D, mhalf[0:R, 0:1], [("nst", b4)], [("nst", b4)])
            return i

        def _norm_p2(i, xin, R, dst, rk, wk, g_ap=None, gk="gbc"):
            b2, b4 = i % 2, i % 8
            if g_ap is None:
                g_ap = gbc[0:R, :]
            rs_ = nst[0:R, b4, 1:2]
            VE("scalar_tensor_tensor", list(rk) + [("nst", b4), gk], [("hn", b2)], out=hn[0:R, b2, :], in0=xin,
               scalar=rs_, in1=g_ap, op0=ALU.mult, op1=ALU.mult)
            for hf in range(2):
                for kk in range(4):
                    k = hf * 4 + kk
                    TR(PT[:, hf, kk, 0:R], hn[0:R, b2, k * 128:(k + 1) * 128], identb[0:R, 0:R],
                       [("hn", b2), "identb"], ["PT"])
                ACT(dst[:, hf * 4:(hf + 1) * 4, :], PT[:, hf, :, 0:R], AF.Copy, ["PT"], wk)

        def normT(xin, R, dst, rk, wk, g_ap=None, gk="gbc"):
            i = _norm_p1(xin, R, rk)
            _norm_p2(i, xin, R, dst, rk, wk, g_ap, gk)

        def normT_batch(items):
            ids = {}
            ids[0] = _norm_p1(items[0][0], items[0][1], items[0][3])
            for n, (xin, R, dst, rk, wk) in enumerate(items):
                if n + 1 < len(items):
                    nx = items[n + 1]
                    ids[n + 1] = _norm_p1(nx[0], nx[1], nx[3])
                with _Unit():
                    _norm_p2(ids[n], xin, R, dst, rk, wk)

        zero4 = sbt(top, "zero4", [4, 128], F32)
        S.add("pool", lambda e: e.memset(zero4[:], 0.0), (), ["zero4"])
        mhalf = sbt(top, "mhalf", [128, 4], F32)
        S.add("pool", lambda e: e.memset(mhalf[:], -0.5), (), ["mhalf"])

        def rstd(out, in_, scale, mh, r, w):
            VE("tensor_scalar", r, w, eng="pool", out=out, in0=in_, scalar1=scale, scalar2=EPS, op0=ALU.mult, op1=ALU.add)
            VE("tensor_tensor", list(w) + ["mhalf"], w, eng="pool", out=out, in0=out, in1=mh, op=ALU.pow)
        c.rstd = rstd
        c.mhalf = mhalf
        for _n, _v in dict(nc=nc, S=S, I=I, O=O, sbt=sbt, MM=MM, TR=TR, ACT=ACT, VE=VE, DMA=DMA, xP=xP, xS=xS,
                           identf=identf, identb=identb, maskst=maskst, onesf=onesf, onesb=onesb, sel4=sel4,
                           selb=selb, gbc=gbc, junk=junk, hn=hn, nst=nst, PS=PS, PT=PT, rot=rot, normT=normT,
                           load_gain=load_gain, zero4=zero4, make_sel16=make_sel16,
                           normT_batch=normT_batch).items():
            setattr(c, _n, _v)
        for l in range(DEPTH):
            if "mix" in stages or "mixs" in stages:
              with ExitStack() as ms:
                Wout = sbt(ms, "Wout", [128, 8, D], BF16)
                wq = sbt(ms, "wq", [128, 4, 128], BF16)
                wk_ = sbt(ms, "wk", [128, 4, 128], BF16)
                wv = sbt(ms, "wv", [128, 4, 128], BF16)
                gvbc = sbt(ms, "gvbc", [128, 512], F32)
                gobc = sbt(ms, "gobc", [128, 512], F32)
                pcol = sbt(ms, "pcol", [128, 24], F32)
                bif = sbt(ms, "bif", [4, 2], F32)
                WsT = sbt(ms, "WsT", [128, 4, 128], BF16)
                bsr = sbt(ms, "bsr", [1, 4, 128], BF16)
                with ExitStack() as wsc:
                    Win = sbt(wsc, "Win", [128, 8, NIN], BF16)
                    pstage = sbt(wsc, "pstage", [24, 128], F32)
                    wsm = sbt(wsc, "wsm", [128, 4, 128], F32)
                    for cb, (c0, c1) in enumerate(((0, 512), (512, 1024), (1024, 1536), (1536, NIN))):
                        DMA(Win[:, :, c0:c1], I["w_in"][l][:, c0:c1].rearrange("(k p) n -> p k n", p=128), [],
                            [("Winc", cb)], eng="pool")
                    load_gain(I["norm_mix_g"][l])
                    DMA(wq[:], I["ml_wq"][l].rearrange("h d e -> d h e"), [], ["wq"], eng="pool")
                    DMA(wk_[:], I["ml_wk"][l].rearrange("h d e -> d h e"), [], ["wk"], eng="pool")
                    DMA(wv[:], I["ml_wv"][l].rearrange("h d e -> d h e"), [], ["wv"], eng="pool")
                    DMA(Wout[:], I["w_out"][l].rearrange("(k p) n -> p k n", p=128), [], ["Wout"], eng="pool")
                    DMA(gvbc[:], I["gm_v_norm_g"][l].partition_broadcast(128), [], ["gvbc"])
                    DMA(gobc[:], I["ml_out_norm_g"][l].partition_broadcast(128), [], ["gobc"])
                    DMA(pstage[0:16, :], I["ml_conv_w"][l].rearrange("j (c p) -> (j c) p", p=128), [], ["pstage"])
                    DMA(pstage[16:20, :], I["ml_conv_b"][l].rearrange("(c p) -> c p", p=128), [], ["pstage"])
                    DMA(pstage[20:24, :], I["ml_skip"][l].rearrange("(c p) -> c p", p=128), [], ["pstage"])
                    DMA(bif[:, 0:1], I["ml_b_i"][l].rearrange("(p o) -> p o", o=1), [], ["bif"])
                    DMA(bif[:, 1:2], I["ml_b_f"][l].rearrange("(p o) -> p o", o=1), [], ["bif"])
                    DMA(wsm[:], I["gm_ws"][l].rearrange("h t s -> t h s"), [], ["wsm"])
                    DMA(bsr[:], I["gm_bs"][l].rearrange("(o h) t -> o h t", o=1), [], ["bsr"], eng="pool")
                    TR(PS[:, 2, 0:24], pstage[0:24, :], identf[0:24, 0:24], ["pstage", "identf"], ["A"])
                    VE("tensor_scalar", ["A"], ["pcol"], out=pcol[:], in0=PS[:, 2, 0:24], scalar1=0.5, scalar2=None, op0=ALU.mult)
                    S.add("pool", lambda e: e.affine_select(out=wsm[:], in_=wsm[:], pattern=[[0, 4], [-1, 128]],
                                                            compare_op=ALU.is_ge, fill=0.0, base=0, channel_multiplier=1),
                          ["wsm"], ["wsm"])
                    for h in range(4):
                        TR(PS[:, 2, 128 * h:128 * (h + 1)], wsm[:, h, :], identf[:], ["wsm", "identf"], ["A"])
                    VE("tensor_copy", ["A"], ["WsT"], out=WsT[:], in_=PS[:, 2, :].rearrange("p (h t) -> p h t", h=4))
                    W = Ctx()
                    for _n, _v in dict(Win=Win, Wout=Wout, wq=wq, wk=wk_, wv=wv, gvbc=gvbc, gobc=gobc, pcol=pcol,
                                       bif=bif, WsT=WsT, bsr=bsr).items():
                        setattr(W, _n, _v)
                    if "mix" in stages:
                      with ExitStack() as ps_:
                        mixer_prompt(c, ps_, l, W)
                        S.flush()
                    S.flush()
                if "mixs" in stages:
                  with ExitStack() as ps_:
                    mixer_sample(c, ps_, l, W)
                    S.flush()
            if "attn" in stages or "attns" in stages:
              with ExitStack() as as_:
                attention(c, as_, l, stages)
                S.flush()
            if "ffn" in stages:
              with ExitStack() as fs:
                ffn(c, fs, l)
                S.flush()

        with ExitStack() as fs:
            yo = sbt(fs, "yo", [128, 2, D], F32)
            load_gain(I["norm_f_g"])
            for t in range(NT + 1):
                R = 128 if t < NT else NS
                xin = xP[:, t, :] if t < NT else xS[:, :]
                xk = [("xP", t)] if t < NT else ["xS"]
                b2, b4 = t % 2, t % 8
                ss = nst[0:R, b4, 0:1]
                rs_ = nst[0:R, b4, 1:2]
                ACT(junk[0:R, jslot(), :], xin, AF.Square, xk, [("junk", _js_cur()), ("nst", b4)], accum_out=ss, saturate=False)
                rstd(rs_, ss, 1.0 / D, mhalf[0:R, 0:1], [("nst", b4)], [("nst", b4)])
                VE("scalar_tensor_tensor", xk + [("nst", b4), "gbc"], [("yo", b2)], out=yo[0:R, b2, :], in0=xin,
                   scalar=rs_, in1=gbc[0:R, :], op0=ALU.mult, op1=ALU.mult)
                dst = O["y_p"][t * 128:(t + 1) * 128, :] if t < NT else O["y_s"]
                DMA(dst, yo[0:R, b2, :], [("yo", b2)], [])
            S.flush(final=True)
    return nc, S


KS = 128.0 ** -0.5
GT = 2
CUT = 99
ACUT = 99
GW = GT * 128


def v22(ap):
    return ap.rearrange("p (a b) -> p a b", a=2)


def mixer_prompt(c, st, l, W):
    S, PS, PT = c.S, c.PS, c.PT
    MM, TR, ACT, VE, DMA, rot, ADD, unit = c.MM, c.TR, c.ACT, c.VE, c.DMA, c.rot, c.ADD, c.unit
    xP, identf, identb = c.xP, c.identf, c.identb
    Win, Wout, pcol = W.Win, W.Wout, W.pcol
    sbt = lambda n, s, d: c.sbt(st, n, s, d)
    hT = sbt("hT", [128, 2, 8, GW], BF16)
    caT = sbt("caT", [128, 2, 4, GW], BF16)
    sigo = sbt("sigo", [128, 2, 4, GW], BF16)
    qT = sbt("qT", [128, 2, 4, GW], BF16)
    kT = sbt("kT", [128, 2, 4, GW], BF16)
    Ktok = sbt("Ktok", [128, 2, GT, 512], BF16)
    cV = sbt("cV", [128, 2, GT, 2, 2, 129], BF16)
    E = sbt("E", [128, 2, GT, 3, 2, 2], F32)
    ebL = sbt("ebL", [128, 2, 2, 2, GT], F32)
    uT = sbt("uT", [128, 4, GW], BF16)
    vtok = sbt("vtok", [128, GT, 512], BF16)
    vg = sbt("vg", [128, 512], F32)
    vst = sbt("vst", [128, GT, 2], F32)
    xmT = sbt("xmT", [128, 4, GW + 3], F32)
    xmb = sbt("xmb", [128, 4, GW], BF16)
    cacc = sbt("cacc", [128, 2, GW], F32)
    tht = sbt("tht", [128, 2, GW], BF16)
    G = sbt("G", [4, 6, GW], F32)
    mc = sbt("mc", [4, 1], F32)
    Cst = sbt("Cst", [128, 2, 2, 129], F32)
    Ctmp = sbt("Ctmp", [128, 2, 2, 129], F32)
    Cbf = sbt("Cbf", [128, 2, 2, 129], BF16)
    ST = sbt("ST", [128, 4, 128], BF16)
    small = sbt("small", [128, 2, 8, 2, 2], F32)
    hct = sbt("hct", [128, 2, 512], BF16)
    ytmp = sbt("ytmp", [128, 4, 128], BF16)
    emf = sbt("emf", [128, 2, 2], F32)
    nflat = sbt("nflat", [128, 2, 2], F32)
    outs = vg

    ADD("pool", lambda e: e.memset(Cst[:], 0.0), (), ["Cst"])
    ADD("pool", lambda e: e.memset(Cbf[:], 0.0), (), ["Cbf"])

    G0, G1, G2, G3, G4, G5 = [G[:, i, :] for i in range(6)]
    gk = lambda i: ("G", i)
    PSB = PS[:, 3:5, :].rearrange("p a (b x) -> p a b x", b=2)
    PSC = PS[:, 5:7, :].rearrange("p a (b x) -> p a b x", b=2)
    NG = NT // GT

    npre, ngm, nproj = [0], [0], [0]

    def stageA(g):
        p = g % 2
        hk = [("hT", p, j) for j in range(GT)]
        c.normT_batch([(xP[:, GT * g + j, :], 128, hT[:, p, :, j * 128:(j + 1) * 128], [("xP", GT * g + j)], [("hT", p, j)])
                       for j in range(GT)])
        npre[0] = len(c.sink)
        for cc in range(4):
            with unit():
                ps, pk = rot()
                ps = ps[:, 0:GW]
                for k in range(8):
                    MM(ps, Win[:, k, cc * 128:(cc + 1) * 128], hT[:, p, k, :], k == 0, k == 7, [("Winc", 0)] + hk, [pk])
                ACT(uT[:, cc, :], ps, AF.Gelu_apprx_tanh, [pk], [("uT", cc)])
        for j in range(GT):
            with unit():
                ps, pk = rot()
                for k in range(8):
                    MM(ps, hT[:, p, k, j * 128:(j + 1) * 128], Win[:, k, 512:1024], k == 0, k == 7,
                       [("Winc", 1), ("hT", p, j)], [pk])
                ACT(vg[:, :], ps, AF.Gelu_apprx_tanh, [pk], ["vg"])
            ACT(c.junk[:, c.jslot(), 0:512], vg[:, :], AF.Square, ["vg"], [("junk", c._js_cur()), ("vst", j)], accum_out=vst[:, j, 0:1], saturate=False)
            c.rstd(vst[:, j, 1:2], vst[:, j, 0:1], 1.0 / 512, c.mhalf[:, 0:1], [("vst", j)], [("vst", j)])
            VE("scalar_tensor_tensor", ["vg", ("vst", j), "gvbc"], [("vtok", j)], out=vtok[:, j, :],
               in0=vg[:, :], scalar=vst[:, j, 1:2], in1=W.gvbc[:], op0=ALU.mult, op1=ALU.mult)
        ngm[0] = len(c.sink)
        for h in range(4):
            with unit():
                ps, pk = rot()
                ps = ps[:, 0:GW]
                for j in range(GT):
                    sl = slice(j * 128, (j + 1) * 128)
                    MM(ps[:, sl], vtok[:, j, h * 128:(h + 1) * 128], W.WsT[:, h, :], True, False, [("vtok", j), "WsT"], [pk])
                    MM(ps[:, sl], c.onesb[0:1, 0:128], W.bsr[0:1, h, :], False, True, ["onesb", "bsr"], [pk])
                VE("tensor_tensor", [pk, ("uT", h)], hk, out=hT[:, p, h, :], in0=ps, in1=uT[:, h, :], op=ALU.mult)

    def stageA2(g):
        p = g % 2
        hk = [("hT", p, j) for j in range(GT)]
        if g == 0:
            ADD("pool", lambda e: e.memset(xmT[:, :, 0:3], 0.0), (), ["xmh"])
        else:
            VE("tensor_copy", [("xmT", cc) for cc in range(4)], ["xmh"], eng="pool", out=xmT[:, :, 0:3],
               in_=xmT[:, :, GW:GW + 3])
        for cc in range(4):
            with unit():
                ps, pk = rot()
                ps = ps[:, 0:GW]
                for k in range(8):
                    MM(ps, Win[:, k, 1024 + cc * 128:1024 + (cc + 1) * 128], hT[:, p, k, :], k == 0, k == 7,
                       [("Winc", 2)] + hk, [pk])
                ACT(xmT[:, cc, 3:GW + 3], ps, AF.Copy, [pk, "xmh"], [("xmT", cc)])
            VE("tensor_copy", [("xmT", cc)], [("xmb", cc)], eng="pool", out=xmb[:, cc, :], in_=xmT[:, cc, 3:GW + 3])
        for cc in range(4):
            with unit():
                ps, pk = rot()
                ps = ps[:, 0:GW]
                for k in range(8):
                    MM(ps, Win[:, k, 1536 + cc * 128:1536 + (cc + 1) * 128], hT[:, p, k, :], k == 0, k == 7,
                       [("Winc", 3)] + hk, [pk])
                ACT(sigo[:, p, cc, :], ps, AF.Tanh, [pk], [("sigo", p, cc)], scale=0.5)
        for q in range(2):
            with unit():
                ps, pk = rot()
                for k in range(8):
                    MM(ps[0:4, 0:GW], Win[:, k, 2048 + 4 * q:2052 + 4 * q], hT[:, p, k, :], k == 0, k == 7,
                       [("Winc", 3)] + hk, [pk])
                ACT(G[:, q, :], ps[0:4, 0:GW], AF.Identity, [pk, "bif"], [gk(q)], bias=W.bif[:, q:q + 1])
        nproj[0] = len(c.sink)
        for cc in range(4):
            b2 = cc % 2
            acc = cacc[:, b2, :]
            VE("tensor_scalar", [("xmT", cc), "xmh", "pcol"], [("cacc", b2)], out=acc, in0=xmT[:, cc, 3:GW + 3],
               scalar1=pcol[:, 12 + cc:13 + cc], scalar2=pcol[:, 16 + cc:17 + cc], op0=ALU.mult, op1=ALU.add)
            for jj in range(3):
                VE("scalar_tensor_tensor", [("xmT", cc), "xmh", "pcol", ("cacc", b2)], [("cacc", b2)], out=acc,
                   in0=xmT[:, cc, jj:jj + GW], scalar=pcol[:, jj * 4 + cc:jj * 4 + cc + 1], in1=acc,
                   op0=ALU.mult, op1=ALU.add)
            ACT(tht[:, b2, :], acc, AF.Tanh, [("cacc", b2)], [("tht", b2)])
            VE("scalar_tensor_tensor", [("tht", b2), ("cacc", b2)], [("caT", p, cc)], out=caT[:, p, cc, :],
               in0=tht[:, b2, :], scalar=1.0, in1=acc, op0=ALU.add, op1=ALU.mult)
        VE("scalar_tensor_tensor", [gk(1)], [gk(2)], out=G2, in0=G1, scalar=-1.0, in1=G1, op0=ALU.mult, op1=ALU.max)
        ACT(G2, G2, AF.Exp, [gk(2)], [gk(2)], scale=-1.0)
        ACT(G2, G2, AF.Ln, [gk(2)], [gk(2)], bias=1.0)
        VE("scalar_tensor_tensor", [gk(1), gk(2)], [gk(1)], out=G1, in0=G1, scalar=0.0, in1=G2,
           op0=ALU.min, op1=ALU.subtract)
        for j in range(GT):
            sl = slice(j * 128, (j + 1) * 128)
            VE("tensor_tensor_scan", [gk(1), "zero4"], [gk(2)], out=G2[:, sl], data0=G1[:, sl],
               data1=c.zero4[:, :], initial=0.0, op0=ALU.add, op1=ALU.add)
        VE("tensor_tensor_scan", [gk(1), gk(0), "mc"], [gk(3)], out=G3, data0=G1, data1=G0,
           initial=(0.0 if g == 0 else mc[:, 0:1]), op0=ALU.add, op1=ALU.max)
        VE("tensor_copy", [gk(3)], ["mc"], out=mc[:, 0:1], in_=G3[:, GW - 1:GW])
        VE("tensor_tensor", [gk(2), gk(3)], [gk(4)], out=G4, in0=G2, in1=G3, op=ALU.subtract)
        VE("tensor_tensor", [gk(0), gk(2)], [gk(0)], out=G0, in0=G0, in1=G2, op=ALU.subtract)
        VE("tensor_scalar", [gk(3)], [gk(5)], out=G5, in0=G3, scalar1=-1.0, scalar2=None, op0=ALU.mult)
        with unit():
            for j in range(GT):
                for q, (Gq, gi) in enumerate(((G4, 4), (G5, 5), (G0, 0))):
                    o = (j * 3 + q) * 4
                    TR(PS[:, 2, o:o + 4], Gq[0:4, j * 128:(j + 1) * 128], identf[0:4, 0:4], [gk(gi), "identf"], ["A"])
            ACT(E[:, p].rearrange("p j q a b -> p (j q a b)"), PS[:, 2, 0:12 * GT], AF.Exp, ["A"], [("E", p)])
        with unit():
            for h in range(4):
                MM(PS[:, 2, 64 + GT * h:64 + GT * (h + 1)], c.sel4[0:4, h, :], G2[0:4, 127:GW:128], True, True,
                   ["sel4", gk(2)], ["A"])
            ACT(ebL[:, p].rearrange("p a b j -> p (a b j)"), PS[:, 2, 64:64 + 4 * GT], AF.Exp, ["A"], [("ebL", p)])
        for h in range(4):
            with unit():
                ps, pk = rot()
                ps = ps[:, 0:GW]
                MM(ps, W.wq[:, h, :], caT[:, p, h, :], True, True, ["wq", ("caT", p, h)], [pk])
                ACT(qT[:, p, h, :], ps, AF.Copy, [pk], [("qT", p, h)])
        for h in range(4):
            with unit():
                ps, pk = rot()
                ps = ps[:, 0:GW]
                MM(ps, W.wk[:, h, :], caT[:, p, h, :], True, True, ["wk", ("caT", p, h)], [pk])
                ACT(kT[:, p, h, :], ps, AF.Copy, [pk], [("kT", p, h)], scale=KS)
        for j in range(GT):
            sl = slice(j * 128, (j + 1) * 128)
            with unit():
                ps, pk = rot()
                for h in range(4):
                    MM(ps[:, h * 128:(h + 1) * 128], caT[:, p, h, sl], W.wk[:, h, :], True, True,
                       [("caT", p, h), "wk"], [pk])
                ACT(Ktok[:, p, j, :], ps, AF.Copy, [pk], [("Ktok", p, j)], scale=KS)
        for j in range(GT):
            sl = slice(j * 128, (j + 1) * 128)
            with unit():
                ps, pk = rot()
                for h in range(4):
                    MM(ps[:, h * 128:(h + 1) * 128], xmb[:, h, sl], W.wv[:, h, :], True, True, [("xmb", h), "wv"], [pk])
                VE("tensor_tensor", [pk, ("E", p)], [("cV", p, j)], out=cV[:, p, j, :, :, 0:128],
                   in0=ps.rearrange("p (a b x) -> p a b x", a=2, b=2),
                   in1=E[:, p, j, 2, :, :].unsqueeze(3).to_broadcast([128, 2, 2, 128]), op=ALU.mult)
            VE("tensor_copy", [("E", p)], [("cV", p, j)], out=cV[:, p, j, :, :, 128], in_=E[:, p, j, 2, :, :])

    def stageB(g):
        p = g % 2
        for j in range(GT):
            sl = slice(j * 128, (j + 1) * 128)
            b2 = j % 2
            smk = ("small", b2)
            sm = lambda i: small[:, b2, i, :, :]
            with unit():
                for h in range(4):
                    MM(PS[:, 2, h * 128:(h + 1) * 128], kT[:, p, h, sl], qT[:, p, h, sl], True, True,
                       [("kT", p, h), ("qT", p, h)], ["A"])
                VE("tensor_tensor", ["A", "maskst"], ["ST"], out=ST[:], in0=PS[:, 2, :].rearrange("p (h t) -> p h t", h=4),
                   in1=c.maskst[:, :].unsqueeze(1).to_broadcast([128, 4, 128]), op=ALU.mult)
            for h in range(4):
                pb = PSB[:, h // 2, h % 2, 0:129]
                MM(pb, ST[:, h, :], cV[:, p, j, h // 2, h % 2, :], True, False, ["ST", ("cV", p, j)], ["B"])
                MM(pb, qT[:, p, h, sl], Cbf[:, h // 2, h % 2, :], False, True, [("qT", p, h), "Cbf"], ["B"])
            rowf = E[:, p, j, 0, :, :]
            emt = E[:, p, j, 1, :, :]
            VE("tensor_tensor", ["B", ("E", p)], [smk], out=sm(0), in0=PSB[:, :, :, 128], in1=rowf, op=ALU.mult)
            VE("scalar_tensor_tensor", [smk], [smk], out=sm(0), in0=sm(0), scalar=-1.0, in1=sm(0), op0=ALU.mult, op1=ALU.max)
            VE("tensor_tensor", [smk, ("E", p)], [smk], out=sm(1), in0=sm(0), in1=emt, op=ALU.max)
            VE("reciprocal", [smk], [smk], out=sm(1), in_=sm(1))
            VE("tensor_tensor", [smk, ("E", p)], [smk], out=sm(2), in0=sm(1), in1=rowf, op=ALU.mult)
            for h in range(4):
                ACT(c.junk[:, c.jslot(), h * 128:(h + 1) * 128], PSB[:, h // 2, h % 2, 0:128], AF.Square, ["B"],
                    [("junk", c._js_cur()), smk], accum_out=small[:, b2, 3, h // 2, h % 2:h % 2 + 1], saturate=False)
            VE("tensor_tensor", [smk], [smk], out=sm(4), in0=sm(2), in1=sm(2), op=ALU.mult)
            VE("tensor_tensor", [smk], [smk], out=sm(4), in0=sm(4), in1=sm(3), op=ALU.mult)
            c.rstd(sm(4), sm(4), 1.0 / 128, v22(c.mhalf[:, 0:4]), [smk], [smk])
            VE("scalar_tensor_tensor", [smk], [smk], out=sm(5), in0=sm(4), scalar=0.5, in1=sm(2), op0=ALU.mult, op1=ALU.mult)
            for h in range(4):
                VE("scalar_tensor_tensor", ["B", smk, "gobc"], [("hct", b2)], out=hct[:, b2, h * 128:(h + 1) * 128],
                   in0=PSB[:, h // 2, h % 2, 0:128], scalar=small[:, b2, 5, h // 2, h % 2:h % 2 + 1],
                   in1=W.gobc[:, h * 128:(h + 1) * 128], op0=ALU.mult, op1=ALU.mult)
            with unit():
                for h in range(4):
                    TR(PT[:, b2, h, :], hct[:, b2, h * 128:(h + 1) * 128], identb[:], [("hct", b2), "identb"], ["PT"])
                for h in range(4):
                    VE("scalar_tensor_tensor", [("caT", p, h), "pcol", "PT"], ["ytmp"], out=ytmp[:, h, :],
                       in0=caT[:, p, h, sl], scalar=pcol[:, 20 + h:21 + h], in1=PT[:, b2, h, :], op0=ALU.mult, op1=ALU.add)
            VE("scalar_tensor_tensor", ["ytmp"] + [("sigo", p, h) for h in range(4)], [("hT", p, j)], out=hT[:, p, 4:8, sl],
               in0=sigo[:, p, :, sl], scalar=1.0, in1=ytmp[:], op0=ALU.add, op1=ALU.mult)
            with unit():
                for h in range(4):
                    MM(PSC[:, h // 2, h % 2, 0:129], Ktok[:, p, j, h * 128:(h + 1) * 128], cV[:, p, j, h // 2, h % 2, :],
                       True, True, [("Ktok", p, j), ("cV", p, j)], ["C"])
                VE("tensor_tensor", ["C", "Cst"], ["Ctmp"], out=Ctmp[:], in0=PSC[:, :, :, 0:129], in1=Cst[:], op=ALU.add)
            VE("tensor_tensor", ["Ctmp", ("ebL", p)], ["Cst"], out=Cst[:], in0=Ctmp[:],
               in1=ebL[:, p, :, :, j].unsqueeze(3).to_broadcast([128, 2, 2, 129]), op=ALU.mult)
            ACT(Cbf[:], Cst[:], AF.Copy, ["Cst"], ["Cbf"])
        for j in range(GT):
            t = GT * g + j
            sl = slice(j * 128, (j + 1) * 128)
            for hf in range(2):
                with unit():
                    ps, pk = rot()
                    for k in range(8):
                        MM(ps, hT[:, p, k, sl], Wout[:, k, hf * 512:(hf + 1) * 512], k == 0, k == 7, [("hT", p, j), "Wout"], [pk])
                    VE("tensor_tensor", [pk, ("xP", t)], [("xP", t)], out=xP[:, t, hf * 512:(hf + 1) * 512], in0=ps,
                       in1=xP[:, t, hf * 512:(hf + 1) * 512], op=ALU.add)

    def gen(fn, g):
        c.sink = []
        fn(g)
        u = c.sink
        c.sink = None
        return u

    ua1, ua2 = gen(stageA, 0), gen(stageA2, 0)
    c.replay(c.merge_units([], ua1, ua2, gates=[(2, 0, 1, npre[0]), (1, ngm[0], 2, nproj[0])]))
    for g in range(NG):
        ub = gen(stageB, g)
        ua1 = gen(stageA, g + 1) if g + 1 < NG else []
        ua2 = gen(stageA2, g + 1) if g + 1 < NG else []
        c.replay(c.merge_units(ub, ua1, ua2, gates=[(2, 0, 1, npre[0]), (1, ngm[0], 2, nproj[0])]))

    O = c.O
    for cc in range(4):
        TR(PS[0:3, 2, cc * 128:(cc + 1) * 128], xmT[:, cc, GW:GW + 3], identf[:], [("xmT", cc), "identf"], ["A"])
    ACT(outs[0:3, :], PS[0:3, 2, :], AF.Copy, ["A"], ["vg"])
    DMA(O["cv_p"][l], outs[0:3, :], ["vg"], [])
    for h in range(4):
        MM(PS[:, 5, h:h + 1], c.sel4[0:4, h, :], G3[0:4, GW - 1:GW], True, True, ["sel4", gk(3)], ["C"])
    ACT(emf[:].rearrange("p a b -> p (a b)"), PS[:, 5, 0:4], AF.Exp, ["C"], ["emf"], scale=-1.0)
    VE("tensor_tensor", ["Cst", "emf"], ["Ctmp"], out=Ctmp[:], in0=Cst[:],
       in1=emf[:].unsqueeze(3).to_broadcast([128, 2, 2, 129]), op=ALU.mult)
    DMA(O["C_p"][l].rearrange("(a b) k v -> k a b v", a=2), Ctmp[:, :, :, 0:128], ["Ctmp"], [])
    VE("tensor_copy", ["Ctmp"], ["nflat"], out=nflat[:], in_=Ctmp[:, :, :, 128])
    TR(PS[0:4, 2, 0:128], nflat[:].rearrange("p a b -> p (a b)"), identf[:], ["nflat", "identf"], ["A"])
    ACT(outs[0:4, 0:128], PS[0:4, 2, 0:128], AF.Copy, ["A"], ["vg"])
    DMA(O["n_p"][l], outs[0:4, 0:128], ["vg"], [])
    DMA(O["m_p"][l].rearrange("(p o) -> p o", o=1), G3[0:4, GW - 1:GW], [gk(3)], [])


def v4(ap, n=128):
    return ap.rearrange("p (h x) -> p h x", h=4)


def bc4(ap, R, n=128):
    return ap.unsqueeze(2).to_broadcast([R, 4, n])


def mixer_sample(c, st, l, W):
    S, PS, PT, I, O = c.S, c.PS, c.PT, c.I, c.O
    MM, TR, ACT, VE, DMA, rot = c.MM, c.TR, c.ACT, c.VE, c.DMA, c.rot
    xS, identf, identb = c.xS, c.identf, c.identb
    Wout = W.Wout
    R = NS
    sbt = lambda n, s, d: c.sbt(st, n, s, d)
    sel16 = c.make_sel16(st)
    hTs = sbt("hTs", [128, 8, R], BF16)
    Wst = sbt("Wst", [128, 2, NIN], BF16)
    us = sbt("us", [R, 512], F32)
    vs = sbt("vs", [R, 512], F32)
    xms = sbt("xms", [R, 512], F32)
    os_ = sbt("os", [R, 512], F32)
    zg = sbt("zg", [R, 8], F32)
    vn = sbt("vn", [R, 512], F32)
    cs = sbt("cs", [R, 3, 512], F32)
    cwbc = sbt("cwbc", [R, 4, 512], F32)
    cbbc = sbt("cbbc", [R, 512], F32)
    skbc = sbt("skbc", [R, 512], F32)
    bibc = sbt("bibc", [R, 4], F32)
    bfbc = sbt("bfbc", [R, 4], F32)
    ws00 = sbt("ws00", [R, 4, 1], F32)
    bs0 = sbt("bs0", [R, 4, 1], F32)
    n0 = sbt("n0", [R, 512], F32)
    m0 = sbt("m0", [R, 4], F32)
    scr = sbt("scr", [R, 5, 512], F32)
    ca_s = sbt("ca_s", [R, 512], F32)
    gs = sbt("gs", [R, 16, 4], F32)
    cab = sbt("cab", [R, 512], BF16)
    xmbs = sbt("xmbs", [R, 512], BF16)
    caTs = sbt("caTs", [128, 4, R], BF16)
    xmTs = sbt("xmTs", [128, 4, R], BF16)
    q_s = sbt("q_s", [R, 512], F32)
    k_s = sbt("k_s", [R, 512], F32)
    v_s = sbt("v_s", [R, 512], F32)
    wv_s = sbt("wv_s", [R, 512], F32)
    qTs = sbt("qTs", [128, 4, R], F32)
    qmask = sbt("qmask", [128, 4, R, R], F32)
    decbc = sbt("decbc", [128, R, 4], F32)
    C0t = sbt("C0t", [128, 2, 2, 4, 128], F32)
    kmask = sbt("kmask", [R, 2, 512], F32)
    ymix_s = sbt("ymix_s", [R, 1024], BF16)
    ymixTs = sbt("ymixTs", [128, 8, R], BF16)

    for sb in range(2):
        DMA(C0t[:, sb], I["sC"][l, 2 * sb:2 * sb + 2].rearrange("s h k v -> k s h v"), [], [("C0t", sb)])
    DMA(cs[:], I["scv"][l], [], ["cs"])
    DMA(n0[:], I["sn"][l], [], ["n0"])
    DMA(m0[:], I["sm"][l], [], ["m0"])
    DMA(cwbc[:], I["ml_conv_w"][l].partition_broadcast(R), [], ["cwbc"])
    DMA(cbbc[:], I["ml_conv_b"][l].partition_broadcast(R), [], ["cbbc"])
    DMA(skbc[:], I["ml_skip"][l].partition_broadcast(R), [], ["skbc"])
    DMA(bibc[:], I["ml_b_i"][l].partition_broadcast(R), [], ["bibc"])
    DMA(bfbc[:], I["ml_b_f"][l].partition_broadcast(R), [], ["bfbc"])
    DMA(ws00[:], I["gm_ws"][l][:, 0, 0:1].partition_broadcast(R), [], ["ws00"], allow_slow_non_contiguous=True)
    DMA(bs0[:], I["gm_bs"][l][:, 0:1].partition_broadcast(R), [], ["bs0"], allow_slow_non_contiguous=True)

    kus, kvs, kxms, kos, kzg = "us", "vs", "xms", "os", "zg"
    c.normT(xS[:, :], R, hTs[:, :, :], ["xS"], ["hTs"])
    cols = [(0, 512), (512, 1024), (1024, 1536), (1536, 2048), (2048, 2056)]
    zkeys = [("PS", 0), ("PS", 1), "A", "B", "B"]
    for k in range(8):
        DMA(Wst[:, k % 2, :], I["w_in"][l, k * 128:(k + 1) * 128, :], [], [("Wst", k % 2)], eng="pool")
        for ci, (c0, c1) in enumerate(cols):
            MM(PS[0:R, ci, 0:c1 - c0], hTs[:, k, :], Wst[:, k % 2, c0:c1], k == 0, k == 7, [("Wst", k % 2), "hTs"], [zkeys[ci]])
    for ci, ((c0, c1), dst, fn, dk) in enumerate(zip(cols, (us, vs, xms, os_, zg),
                                                  (AF.Gelu_apprx_tanh, AF.Gelu_apprx_tanh, AF.Copy, AF.Tanh, AF.Copy),
                                                  (kus, kvs, kxms, kos, kzg))):
        ACT(dst[:, :], PS[0:R, ci, 0:c1 - c0], fn, [zkeys[ci]], [dk], **({"scale": 0.5} if dk == "os" else {}))
    g_ = lambda i: gs[:, i, :]
    gkey = lambda i: ("gs", i)
    ACT(c.junk[0:R, c.jslot(), 0:512], vs[:, :], AF.Square, [kvs], [("junk", c._js_cur()), gkey(0)], accum_out=gs[:, 0, 0:1], saturate=False)
    c.rstd(gs[:, 0, 1:2], gs[:, 0, 0:1], 1.0 / 512, c.mhalf[0:R, 0:1], [gkey(0)], [gkey(0)])
    VE("scalar_tensor_tensor", [kvs, gkey(0), "gvbc"], ["vn"], out=vn[:, :], in0=vs[:, :], scalar=gs[:, 0, 1:2],
       in1=W.gvbc[0:R, :], op0=ALU.mult, op1=ALU.mult)
    DMA(O["gv_s"][l], vn[:, :], ["vn"], [], eng="pool")
    s0, s1, s2, s3, s4 = [scr[:, i, :] for i in range(5)]
    sk = lambda i: ("scr", i)
    VE("tensor_tensor", ["vn", "ws00"], [sk(0)], out=v4(s0), in0=v4(vn[:, :]),
       in1=ws00[:, :, :].to_broadcast([R, 4, 128]), op=ALU.mult)
    VE("tensor_tensor", [sk(0), "bs0"], [sk(0)], out=v4(s0), in0=v4(s0), in1=bs0[:, :, :].to_broadcast([R, 4, 128]),
       op=ALU.add)
    VE("tensor_tensor", [sk(0), kus], ["ymg"], out=ymix_s[:, 0:512], in0=s0, in1=us[:, :], op=ALU.mult)
    VE("tensor_tensor", [kxms, "cwbc"], [sk(1)], out=s1, in0=xms[:, :], in1=cwbc[:, 3, :], op=ALU.mult)
    for jj in range(3):
        VE("tensor_tensor", ["cs", "cwbc"], [sk(2)], out=s2, in0=cs[:, jj, :], in1=cwbc[:, jj, :], op=ALU.mult)
        VE("tensor_tensor", [sk(1), sk(2)], [sk(1)], out=s1, in0=s1, in1=s2, op=ALU.add)
    VE("tensor_tensor", [sk(1), "cbbc"], [sk(1)], out=s1, in0=s1, in1=cbbc[:, :], op=ALU.add)
    ACT(s2, s1, AF.Tanh, [sk(1)], [sk(2)], scale=0.5)
    VE("scalar_tensor_tensor", [sk(1), sk(2)], ["ca_s"], out=ca_s[:, :], in0=s2, scalar=1.0, in1=s1, op0=ALU.add, op1=ALU.mult)
    VE("tensor_scalar", ["ca_s"], ["ca_s"], out=ca_s[:, :], in0=ca_s[:, :], scalar1=0.5, scalar2=None, op0=ALU.mult)
    DMA(O["cv_s"][l][:, 0:2, :], cs[:, 1:3, :], ["cs"], [], eng="pool")
    DMA(O["cv_s"][l][:, 2, :], xms[:, :], [kxms], [], eng="pool")
    VE("tensor_tensor", [kzg, "bibc"], [gkey(1)], out=g_(1), in0=zg[:, 0:4], in1=bibc[:, :], op=ALU.add)
    VE("tensor_tensor", [kzg, "bfbc"], [gkey(2)], out=g_(2), in0=zg[:, 4:8], in1=bfbc[:, :], op=ALU.add)
    VE("scalar_tensor_tensor", [gkey(2)], [gkey(3)], out=g_(3), in0=g_(2), scalar=-1.0, in1=g_(2), op0=ALU.mult, op1=ALU.max)
    ACT(g_(3), g_(3), AF.Exp, [gkey(3)], [gkey(3)], scale=-1.0)
    ACT(g_(3), g_(3), AF.Ln, [gkey(3)], [gkey(3)], bias=1.0)
    VE("scalar_tensor_tensor", [gkey(2), gkey(3)], [gkey(2)], out=g_(2), in0=g_(2), scalar=0.0, in1=g_(3),
       op0=ALU.min, op1=ALU.subtract)
    VE("tensor_tensor", [gkey(2), "m0"], [gkey(3)], out=g_(3), in0=g_(2), in1=m0[:, :], op=ALU.add)
    VE("tensor_tensor", [gkey(3), gkey(1)], [gkey(4)], out=g_(4), in0=g_(3), in1=g_(1), op=ALU.max)
    DMA(O["m_s"][l], g_(4), [gkey(4)], [], eng="pool")
    VE("tensor_tensor", [gkey(3), gkey(4)], [gkey(5)], out=g_(5), in0=g_(3), in1=g_(4), op=ALU.subtract)
    ACT(g_(5), g_(5), AF.Exp, [gkey(5)], [gkey(5)])
    VE("tensor_tensor", [gkey(1), gkey(4)], [gkey(6)], out=g_(6), in0=g_(1), in1=g_(4), op=ALU.subtract)
    ACT(g_(6), g_(6), AF.Exp, [gkey(6)], [gkey(6)])
    ACT(g_(7), g_(4), AF.Exp, [gkey(4)], [gkey(7)], scale=-1.0)
    VE("tensor_copy", ["ca_s"], ["cab"], out=cab[:, :], in_=ca_s[:, :])
    VE("tensor_copy", [kxms], ["xmbs"], out=xmbs[:, :], in_=xms[:, :])
    for hf, (src, srck, dstT, dk) in enumerate(((cab, "cab", caTs, "caTs"), (xmbs, "xmbs", xmTs, "xmTs"))):
        for h in range(4):
            TR(PT[:, hf, h, 0:R], src[0:R, h * 128:(h + 1) * 128], identb[0:R, 0:R], [srck, "identb"], ["PT"])
        ACT(dstT[:, :, :], PT[:, hf, :, 0:R], AF.Copy, ["PT"], [dk])
    for w_, wkey, srcT, srck, dst, dk, scl in ((W.wq, "wq", caTs, "caTs", q_s, "q_s", 1.0),
                                                (W.wk, "wk", caTs, "caTs", k_s, "k_s", KS),
                                                (W.wv, "wv", xmTs, "xmTs", v_s, "v_s", 1.0)):
        ps, pk = rot()
        for h in range(4):
            MM(ps[0:R, h * 128:(h + 1) * 128], srcT[:, h, :], w_[:, h, :], True, True, [srck, wkey], [pk])
        ACT(dst[:, :], ps[0:R, :], AF.Copy, [pk], [dk], scale=scl)
    VE("tensor_tensor", ["q_s", "k_s"], [sk(2)], out=s2, in0=q_s[:, :], in1=k_s[:, :], op=ALU.mult)
    VE("tensor_reduce", [sk(2)], [gkey(8)], out=g_(8), in_=v4(s2), axis=AX.X, op=ALU.add)
    VE("tensor_tensor", ["q_s", "n0"], [sk(2)], out=s2, in0=q_s[:, :], in1=n0[:, :], op=ALU.mult)
    VE("tensor_reduce", [sk(2)], [gkey(9)], out=g_(9), in_=v4(s2), axis=AX.X, op=ALU.add)
    VE("tensor_tensor", [gkey(8), gkey(6)], [gkey(10)], out=g_(10), in0=g_(8), in1=g_(6), op=ALU.mult)
    VE("tensor_tensor", [gkey(9), gkey(5)], [gkey(11)], out=g_(11), in0=g_(9), in1=g_(5), op=ALU.mult)
    VE("tensor_tensor", [gkey(11), gkey(10)], [gkey(11)], out=g_(11), in0=g_(11), in1=g_(10), op=ALU.add)
    VE("scalar_tensor_tensor", [gkey(11)], [gkey(11)], out=g_(11), in0=g_(11), scalar=-1.0, in1=g_(11), op0=ALU.mult, op1=ALU.max)
    VE("tensor_tensor", [gkey(11), gkey(7)], [gkey(11)], out=g_(11), in0=g_(11), in1=g_(7), op=ALU.max)
    VE("reciprocal", [gkey(11)], [gkey(11)], out=g_(11), in_=g_(11))
    for h in range(4):
        TR(PS[:, 2, h * R:(h + 1) * R], q_s[0:R, h * 128:(h + 1) * 128], identf[0:R, 0:R], ["q_s", "identf"], ["A"])
    VE("tensor_copy", ["A"], ["qTs"], out=qTs[:, :, :], in_=PS[:, 2, 0:4 * R].rearrange("p (h s) -> p h s", h=4))
    for h in range(4):
        VE("tensor_tensor", ["qTs", "selb"], ["qmask"], out=qmask[:, h, :, :],
           in0=qTs[:, h, :].unsqueeze(1).to_broadcast([128, R, R]), in1=c.selb[:, :, :], op=ALU.mult)
    for s in range(R):
        MM(PS[:, 2, 64 + 4 * s:68 + 4 * s], sel16[0:R, s, :], g_(5), True, True, ["sel16", gkey(5)], ["A"])
    VE("tensor_copy", ["A"], ["decbc"], out=decbc[:, :, :], in_=PS[:, 2, 64:128].rearrange("p (s h) -> p s h", h=4))
    VE("tensor_tensor", ["v_s", gkey(6)], ["wv_s"], out=v4(wv_s[:, :]), in0=v4(v_s[:, :]), in1=bc4(g_(6), R), op=ALU.mult)
    VE("tensor_tensor", ["n0", gkey(5)], [sk(3)], out=v4(s3), in0=v4(n0[:, :]), in1=bc4(g_(5), R), op=ALU.mult)
    VE("tensor_tensor", ["k_s", gkey(6)], [sk(4)], out=v4(s4), in0=v4(k_s[:, :]), in1=bc4(g_(6), R), op=ALU.mult)
    VE("tensor_tensor", [sk(3), sk(4)], [sk(3)], out=s3, in0=s3, in1=s4, op=ALU.add)
    DMA(O["n_s"][l], s3, [sk(3)], [], eng="pool")
    for sb in range(R // 2):
        b2 = sb % 2
        sA = sb * 2
        if sb >= 2:
            DMA(C0t[:, b2], I["sC"][l, sA:sA + 2].rearrange("s h k v -> k s h v"), [], [("C0t", b2)])
        VE("tensor_tensor", ["k_s", "identf"], ["kmask"], out=kmask[:, :, :],
           in0=k_s[:, :].unsqueeze(1).to_broadcast([R, 2, 512]),
           in1=identf[0:R, sA:sA + 2].unsqueeze(2).to_broadcast([R, 2, 512]), op=ALU.mult)
        for si in range(2):
            s = sA + si
            for h in range(4):
                MM(PS[0:R, 3 + h, 0:128], qmask[:, h, s, :], C0t[:, b2, si, h, :], s == 0, s == R - 1,
                   ["qmask", ("C0t", b2)], [("qC", h)])
        for si in range(2):
            for h in range(4):
                MM(PS[:, si, h * 128:(h + 1) * 128], kmask[:, si, h * 128:(h + 1) * 128],
                   wv_s[0:R, h * 128:(h + 1) * 128], True, True, ["kmask", "wv_s"], [("PS", si)])
        cview = C0t[:, b2, :, :, :]
        VE("tensor_tensor", [("C0t", b2), "decbc"], [("C0t", b2)], out=cview, in0=cview,
           in1=decbc[:, sA:sA + 2, :].unsqueeze(3).to_broadcast([128, 2, 4, 128]), op=ALU.mult)
        VE("tensor_tensor", [("C0t", b2), ("PS", 0), ("PS", 1)], [("C0t", b2)], out=cview, in0=cview,
           in1=PS[:, 0:2, :].rearrange("p s (h v) -> p s h v", h=4), op=ALU.add)
        DMA(O["C_s"][l, sA:sA + 2].rearrange("s h k v -> k s h v"), C0t[:, b2], [("C0t", b2)], [], eng="pool")
    qck = [("qC", h) for h in range(4)]
    VE("tensor_tensor", ["v_s", gkey(10)], [sk(0)], out=v4(s0), in0=v4(v_s[:, :]), in1=bc4(g_(10), R), op=ALU.mult)
    VE("tensor_tensor", qck + [gkey(5)], [sk(1)], out=v4(s1), in0=PS[0:R, 3:7, 0:128], in1=bc4(g_(5), R), op=ALU.mult)
    VE("tensor_tensor", [sk(0), sk(1)], [sk(0)], out=s0, in0=s0, in1=s1, op=ALU.add)
    VE("tensor_tensor", [sk(0), gkey(11)], [sk(0)], out=v4(s0), in0=v4(s0), in1=bc4(g_(11), R), op=ALU.mult)
    VE("tensor_tensor", [sk(0)], [sk(1)], out=s1, in0=s0, in1=s0, op=ALU.mult)
    VE("tensor_reduce", [sk(1)], [gkey(12)], out=g_(12), in_=v4(s1), axis=AX.X, op=ALU.add)
    c.rstd(g_(12), g_(12), 1.0 / 128, c.mhalf[0:R, 0:4], [gkey(12)], [gkey(12)])
    VE("tensor_tensor", [sk(0), gkey(12)], [sk(0)], out=v4(s0), in0=v4(s0), in1=bc4(g_(12), R), op=ALU.mult)
    VE("tensor_tensor", [sk(0), "gobc"], [sk(0)], out=s0, in0=s0, in1=W.gobc[0:R, :], op=ALU.mult)
    VE("tensor_tensor", ["ca_s", "skbc"], [sk(1)], out=s1, in0=ca_s[:, :], in1=skbc[:, :], op=ALU.mult)
    VE("tensor_tensor", [sk(0), sk(1)], [sk(0)], out=s0, in0=s0, in1=s1, op=ALU.add)
    VE("tensor_scalar", [sk(0)], [sk(0)], out=s0, in0=s0, scalar1=0.5, scalar2=None, op0=ALU.mult)
    VE("scalar_tensor_tensor", [sk(0), kos], ["yml"], out=ymix_s[:, 512:1024], in0=os_[:, :], scalar=1.0, in1=s0,
       op0=ALU.add, op1=ALU.mult)
    for hf in range(2):
        for kk in range(4):
            k = hf * 4 + kk
            TR(PT[:, hf, kk, 0:R], ymix_s[0:R, k * 128:(k + 1) * 128], identb[0:R, 0:R], ["ymg", "yml", "identb"],
               ["PT"])
        ACT(ymixTs[:, hf * 4:(hf + 1) * 4, :], PT[:, hf, :, 0:R], AF.Copy, ["PT"], [("ymixTs", hf)])
    for hf in range(2):
        ps, pk = rot()
        for k in range(8):
            MM(ps[0:R, :], ymixTs[:, k, :], Wout[:, k, hf * 512:(hf + 1) * 512], k == 0, k == 7,
               [("ymixTs", 0), ("ymixTs", 1), "Wout"], [pk])
        VE("tensor_tensor", [pk, "xS"], ["xS"], out=xS[:, hf * 512:(hf + 1) * 512], in0=ps[0:R, :],
           in1=xS[:, hf * 512:(hf + 1) * 512], op=ALU.add)


def attention(c, st, l, stages):
    S, PS, PT, I, O = c.S, c.PS, c.PT, c.I, c.O
    MM, TR, ACT, VE, DMA, rot = c.MM, c.TR, c.ACT, c.VE, c.DMA, c.rot
    xP, xS, identf, identb = c.xP, c.xS, c.identf, c.identb
    sbt = lambda n, s, d: c.sbt(st, n, s, d)
    R = NS
    SC = 1.0 / 16.0
    Wcq = sbt("Wcq", [128, 8, D], BF16)
    Wco = sbt("Wco", [128, 8, D], BF16)
    mkT = sbt("mkT", [128, 8, MEM], BF16)
    mvb = sbt("mvb", [128, 2, D], BF16)
    if "attn" not in stages:
        DMA(Wcq[:], I["w_cq"][l].rearrange("(k p) n -> p k n", p=128), [], ["Wcq"], eng="pool")
        DMA(Wco[:], I["w_co"][l].rearrange("(k p) n -> p k n", p=128), [], ["Wco"], eng="pool")
    if "attn" in stages:
      with ExitStack() as s0:
        sb0 = lambda n, s, d: c.sbt(s0, n, s, d)
        Wck = sb0("Wck", [128, 8, D], BF16)
        Wcv = sb0("Wcv", [128, 8, D], BF16)
        memt = sb0("memt", [128, 2, D], F32)
        mnT = sb0("mnT", [128, 8, MEM], BF16)
        kvo = sb0("kvo", [128, 2, 512], F32)
        DMA(Wck[:], I["w_ck"][l].rearrange("(k p) n -> p k n", p=128), [], ["Wck"], eng="pool")
        DMA(Wcv[:], I["w_cv"][l].rearrange("(k p) n -> p k n", p=128), [], ["Wcv"], eng="pool")
        DMA(Wcq[:], I["w_cq"][l].rearrange("(k p) n -> p k n", p=128), [], ["Wcq"], eng="pool")
        DMA(Wco[:], I["w_co"][l].rearrange("(k p) n -> p k n", p=128), [], ["Wco"], eng="pool")
        DMA(memt[:], I["mem"].rearrange("(t p) d -> p t d", p=128), [], ["memt"])
        c.load_gain(I["norm_mem_g"][l])
        for mt in range(2):
            c.normT(memt[:, mt, :], 128, mnT[:, :, mt * 128:(mt + 1) * 128], ["memt"], [("mnT", mt)])
        mnk = [("mnT", 0), ("mnT", 1)]
        oi = 0
        if ACUT <= 0.3:
            S.flush()
            return
        for Wt, wkey, outn in ((Wck, "Wck", "mk_o"), (Wcv, "Wcv", "mv_o")):
            for mt in range(2):
                for hf in range(2):
                    ps, pk = rot()
                    for k in range(8):
                        MM(ps, mnT[:, k, mt * 128:(mt + 1) * 128], Wt[:, k, hf * 512:(hf + 1) * 512], k == 0, k == 7,
                           [("mnT", mt), wkey], [pk])
                    b2 = oi % 2
                    oi += 1
                    ACT(kvo[:, b2, :], ps, AF.Copy, [pk], [("kvo", b2)])
                    DMA(O[outn][l, mt * 128:(mt + 1) * 128, hf * 512:(hf + 1) * 512], kvo[:, b2, :], [("kvo", b2)], [])
                    if outn == "mv_o":
                        VE("tensor_copy", [("kvo", b2)], [("mvb", mt)], out=mvb[:, mt, hf * 512:(hf + 1) * 512],
                           in_=kvo[:, b2, :])
        if ACUT <= 0.6:
            S.flush()
            return
        for cc in range(8):
            ps, pk = rot()
            for k in range(8):
                MM(ps[:, 0:MEM], Wck[:, k, cc * 128:(cc + 1) * 128], mnT[:, k, :], k == 0, k == 7, ["Wck"] + mnk, [pk])
            ACT(mkT[:, cc, :], ps[:, 0:MEM], AF.Copy, [pk], [("mkT", cc)])
        S.flush()
    with ExitStack() as s1:
        sb1 = lambda n, s, d: c.sbt(s1, n, s, d)
        do_p = "attn" in stages
        do_s = "attns" in stages
        c.load_gain(I["norm_xa_g"][l])
        if do_p:
            hxT = sb1("hxT", [128, 8, 512], BF16)
            hqT = sb1("hqT", [128, 8, 512], BF16)
            Pun = sb1("Pun", [128, 2, 4, MEM], BF16)
            Pn = sb1("Pn", [128, 2, 4, MEM], BF16)
            PTs = sb1("PTs", [128, 8, 512], BF16)
            sms = sb1("sms", [128, 2, 4, 4], F32)
        if do_s:
            sel16 = c.make_sel16(s1, BF16)
            hxTs = sb1("hxTs", [128, 8, R], BF16)
            hq_s = sb1("hq_s", [R, D], BF16)
            KV = sb1("KV", [128, 2, 2, D], F32)
            KVb = sb1("KVb", [128, 2, 2, D], BF16)
            qb = sb1("qb", [128, D], F32)
            prod = sb1("prod", [128, D], F32)
            scT = sb1("scT", [128, 2, R, 4], F32)
            sc_sb = sb1("sc_sb", [64, MEM], F32)
            p_sb = sb1("p_sb", [64, MEM], F32)
            sst = sb1("sst", [64, 4], F32)
            pT = sb1("pT", [128, 2, R, 4], F32)
            Pm = sb1("Pm", [128, 2, 4, R, R], BF16)
            att_s = sb1("att_s", [R, D], BF16)
            attTs = sb1("attTs", [128, 8, R], BF16)
        unit = c.unit

        def prompt_stream():
            hxk = [("hxT", j) for j in range(4)]
            for g in range(4):
                c.normT_batch([(xP[:, 4 * g + j, :], 128, hxT[:, :, j * 128:(j + 1) * 128], [("xP", 4 * g + j)], [("hxT", j)])
                               for j in range(4)])
                for cc in range(8):
                    with unit():
                        ps, pk = rot()
                        for k in range(8):
                            MM(ps, Wcq[:, k, cc * 128:(cc + 1) * 128], hxT[:, k, :], k == 0, k == 7, ["Wcq"] + hxk, [pk])
                        ACT(hqT[:, cc, :], ps, AF.Copy, [pk], [("hqT", cc)])
                def tile_stream(js):
                    for j in js:
                        sl = slice(j * 128, (j + 1) * 128)
                        b2 = j % 2
                        smk = ("sms", b2)
                        for hp in range(2):
                            with unit():
                                for hh in range(2):
                                    h = hp * 2 + hh
                                    for dc in range(2):
                                        MM(PS[:, 3, hh * MEM:(hh + 1) * MEM], hqT[:, 2 * h + dc, sl], mkT[:, 2 * h + dc, :],
                                           dc == 0, dc == 1, [("hqT", 2 * h + dc), ("mkT", 2 * h + dc)], ["B3"])
                                for hh in range(2):
                                    h = hp * 2 + hh
                                    VE("tensor_scalar", ["B3"], [("Pun", b2, h), smk], out=Pun[:, b2, h, :],
                                       in0=PS[:, 3, hh * MEM:(hh + 1) * MEM], scalar1=1.0, scalar2=-3.0e38, op0=ALU.mult,
                                       op1=ALU.max, accum_out=sms[:, b2, 0, h:h + 1])
                                VE("tensor_scalar", [smk], [smk], out=sms[:, b2, 1, hp * 2:hp * 2 + 2],
                                   in0=sms[:, b2, 0, hp * 2:hp * 2 + 2], scalar1=-SC, scalar2=None, op0=ALU.mult)
                                for hh in range(2):
                                    h = hp * 2 + hh
                                    ACT(Pun[:, b2, h, :], PS[:, 3, hh * MEM:(hh + 1) * MEM], AF.Exp, ["B3", smk],
                                        [("Pun", b2, h), smk], scale=SC, bias=sms[:, b2, 1, h:h + 1],
                                        accum_out=sms[:, b2, 2, h:h + 1])
                        VE("reciprocal", [smk], [smk], out=sms[:, b2, 3, :], in_=sms[:, b2, 2, :])
                        VE("tensor_tensor", [("Pun", b2, h) for h in range(4)] + [smk], [("Pn", b2)], out=Pn[:, b2, :, :],
                           in0=Pun[:, b2, :, :], in1=sms[:, b2, 3, :].unsqueeze(2).to_broadcast([128, 4, MEM]), op=ALU.mult)
                        with unit():
                            for h in range(4):
                                for mt in range(2):
                                    TR(PT[:, h // 2, (h % 2) * 2 + mt, :], Pn[:, b2, h, mt * 128:(mt + 1) * 128], identb[:],
                                       [("Pn", b2), "identb"], ["PT"])
                            for hf in range(2):
                                ACT(PTs[:, hf * 4:(hf + 1) * 4, sl], PT[:, hf, :, :], AF.Copy, ["PT"], [("PTs", j)])

                outer = c.sink
                c.sink = []
                tile_stream((0, 2))
                t0 = c.sink
                c.sink = []
                tile_stream((1, 3))
                t1 = c.sink
                c.sink = outer
                c.sink.extend(c.merge_units(t0, t1))
                for cc in range(8):
                    h, dc = cc // 2, cc % 2
                    with unit():
                        ps, pk = rot()
                        for mt in range(2):
                            MM(ps, mvb[:, mt, h * 256 + dc * 128:h * 256 + (dc + 1) * 128], PTs[:, 2 * h + mt, :], mt == 0, mt == 1,
                               [("mvb", mt)] + [("PTs", j) for j in range(4)], [pk])
                        ACT(hxT[:, cc, :], ps, AF.Copy, [pk], hxk)
                for j in range(4):
                    t = 4 * g + j
                    sl = slice(j * 128, (j + 1) * 128)
                    for hf in range(2):
                        with unit():
                            ps, pk = rot()
                            for k in range(8):
                                MM(ps, hxT[:, k, sl], Wco[:, k, hf * 512:(hf + 1) * 512], k == 0, k == 7, [("hxT", j), "Wco"], [pk])
                            VE("tensor_tensor", [pk, ("xP", t)], [("xP", t)], out=xP[:, t, hf * 512:(hf + 1) * 512], in0=ps,
                               in1=xP[:, t, hf * 512:(hf + 1) * 512], op=ALU.add)

        def sample_stream():
            with unit():
                c.normT(xS[:, :], R, hxTs[:, :, :], ["xS"], ["hxTs"])
            for hf in range(2):
                with unit():
                    ps, pk = rot()
                    for k in range(8):
                        MM(ps[0:R, :], hxTs[:, k, :], Wcq[:, k, hf * 512:(hf + 1) * 512], k == 0, k == 7, ["hxTs", "Wcq"], [pk])
                    ACT(hq_s[:, hf * 512:(hf + 1) * 512], ps[0:R, :], AF.Copy, [pk], ["hq_s"], scale=SC)
            for s in range(R):
                b2 = s % 2
                DMA(KV[:, b2], I["ck"][l, s].rearrange("(t p) d -> p t d", p=128), [], [("KV", b2)])
                with unit():
                    for hf in range(2):
                        MM(PS[:, hf, :], sel16[0:R, s, :], hq_s[0:R, hf * 512:(hf + 1) * 512], True, True,
                           ["sel16", "hq_s"], [("PS", hf)])
                    ACT(qb[:, :], PS[:, 0:2, :].rearrange("p a x -> p (a x)"), AF.Copy, [("PS", 0), ("PS", 1)], ["qb"])
                for mt in range(2):
                    VE("tensor_tensor", [("KV", b2), "qb"], ["prod"], eng="pool", out=prod[:, :],
                       in0=KV[:, b2, mt, :], in1=qb[:, :], op=ALU.mult)
                    VE("tensor_reduce", ["prod"], ["scT"], out=scT[:, mt, s, :],
                       in_=prod[:, :].rearrange("p (h d) -> p h d", h=4), axis=AX.X, op=ALU.add)
            with unit():
                for mt in range(2):
                    TR(PS[0:64, 2, mt * 128:(mt + 1) * 128], scT[:, mt, :, :].rearrange("p s h -> p (s h)"), identf[:],
                       ["scT", "identf"], ["A"])
                VE("tensor_copy", ["A"], ["sc_sb"], out=sc_sb[:, :], in_=PS[0:64, 2, 0:MEM])
            VE("tensor_scalar", ["sc_sb"], ["sst", "p_sb"], out=p_sb[:, :], in0=sc_sb[:, :], scalar1=1.0, scalar2=-3.0e38,
               op0=ALU.mult, op1=ALU.max, accum_out=sst[:, 0:1])
            VE("tensor_scalar", ["sst"], ["sst"], out=sst[:, 1:2], in0=sst[:, 0:1], scalar1=-1.0, scalar2=None, op0=ALU.mult)
            ACT(p_sb[:, :], sc_sb[:, :], AF.Exp, ["sc_sb", "sst"], ["p_sb", "sst"], bias=sst[:, 1:2], accum_out=sst[:, 2:3])
            VE("reciprocal", ["sst"], ["sst"], out=sst[:, 3:4], in_=sst[:, 2:3])
            VE("tensor_scalar", ["p_sb", "sst"], ["p_sb"], out=p_sb[:, :], in0=p_sb[:, :], scalar1=sst[:, 3:4], scalar2=None,
               op0=ALU.mult)
            with unit():
                for mt in range(2):
                    TR(PS[:, 2, mt * 64:(mt + 1) * 64], p_sb[0:64, mt * 128:(mt + 1) * 128], identf[0:64, 0:64],
                       ["p_sb", "identf"], ["A"])
                VE("tensor_copy", ["A"], ["pT"], out=pT[:].rearrange("p a s h -> p a (s h)"),
                   in_=PS[:, 2, 0:128].rearrange("p (a x) -> p a x", a=2))
            for mt in range(2):
                for h in range(4):
                    VE("tensor_tensor", ["pT", "selb"], ["Pm"], out=Pm[:, mt, h, :, :],
                       in0=pT[:, mt, :, h].unsqueeze(1).to_broadcast([128, R, R]), in1=c.selb[:, :, :], op=ALU.mult)
            accb = [2, 4, 5, 6]
            acck = ["A", ("acc", 4), ("acc", 5), ("acc", 6)]
            for s in range(R):
                b2 = s % 2
                DMA(KVb[:, b2], I["cv"][l, s].rearrange("(t p) d -> p t d", p=128), [], [("KVb", b2)], eng="pool")
                for mt in range(2):
                    for h in range(4):
                        MM(PS[0:R, accb[h], 0:256], Pm[:, mt, h, s, :], KVb[:, b2, mt, h * 256:(h + 1) * 256],
                           s == 0 and mt == 0, s == R - 1 and mt == 1, ["Pm", ("KVb", b2)], [acck[h]])
            for h in range(4):
                ACT(att_s[:, h * 256:(h + 1) * 256], PS[0:R, accb[h], 0:256], AF.Copy, [acck[h]], ["att_s"])
            for hf in range(2):
                with unit():
                    for kk in range(4):
                        k = hf * 4 + kk
                        TR(PT[:, hf, kk, 0:R], att_s[0:R, k * 128:(k + 1) * 128], identb[0:R, 0:R], ["att_s", "identb"], ["PT"])
                    ACT(attTs[:, hf * 4:(hf + 1) * 4, :], PT[:, hf, :, 0:R], AF.Copy, ["PT"], [("attTs", hf)])
            for hf in range(2):
                with unit():
                    ps, pk = rot()
                    for k in range(8):
                        MM(ps[0:R, :], attTs[:, k, :], Wco[:, k, hf * 512:(hf + 1) * 512], k == 0, k == 7,
                           [("attTs", 0), ("attTs", 1), "Wco"], [pk])
                    VE("tensor_tensor", [pk, "xS"], ["xS"], out=xS[:, hf * 512:(hf + 1) * 512], in0=ps[0:R, :],
                       in1=xS[:, hf * 512:(hf + 1) * 512], op=ALU.add)

        def gen(fn):
            c.sink = []
            fn()
            u = c.sink
            c.sink = None
            return u
        up = gen(prompt_stream) if do_p else []
        us_ = gen(sample_stream) if do_s else []
        c.replay(c.merge_units(up, us_))


def ffn(c, st, l):
    S, PS, PT, I, O = c.S, c.PS, c.PT, c.I, c.O
    MM, TR, ACT, VE, DMA, rot = c.MM, c.TR, c.ACT, c.VE, c.DMA, c.rot
    xP, xS = c.xP, c.xS
    sbt = lambda n, s, d: c.sbt(st, n, s, d)
    R = NS
    NTOK = 1024 + R
    hfT = sbt("hfT", [128, 8, NTOK], BF16)
    aT = sbt("aT", [128, 2, 8, NTOK], BF16)
    Wup = sbt("Wup", [128, 2, 8, 1024], BF16)
    Wdn = sbt("Wdn", [128, 2, 8, 1024], BF16)
    rl = sbt("rl", [128, 2, 512], F32)
    c.load_gain(I["norm_ffn_g"][l])
    wi = 0
    ri = 0
    for half in range(2):
        ntile = 8
        items = [(xP[:, half * 8 + j, :], 128, hfT[:, :, j * 128:(j + 1) * 128], [("xP", half * 8 + j)], [("hfT", j)])
                 for j in range(ntile)]
        if half == 1:
            items.append((xS[:, :], R, hfT[:, :, 1024:1024 + R], ["xS"], [("hfT", 8)]))
        c.normT_batch(items)
        ntiles = [(0, 512, [0, 1, 2, 3]), (512, 1024, [4, 5, 6, 7])] + ([(1024, 1024 + R, [8])] if half == 1 else [])
        for fq in range(4):
            wb = wi % 2
            wi += 1
            DMA(Wup[:, wb], I["w_up"][l][:, fq * 1024:(fq + 1) * 1024].rearrange("(k p) n -> p k n", p=128), [],
                [("Wup", wb)], eng="pool")
            DMA(Wdn[:, wb], I["w_down"][l][fq * 1024:(fq + 1) * 1024, :].rearrange("(k p) n -> p k n", p=128), [],
                [("Wdn", wb)], eng="pool")
            for fc in range(8):
                for (n0_, n1_, tl) in ntiles:
                    ps, pk = rot()
                    n = n1_ - n0_
                    for k in range(8):
                        MM(ps[:, 0:n], Wup[:, wb, k, fc * 128:(fc + 1) * 128], hfT[:, k, n0_:n1_], k == 0, k == 7,
                           [("Wup", wb)] + [("hfT", j) for j in tl], [pk])
                    rb = ri % 2
                    ri += 1
                    ACT(rl[:, rb, 0:n], ps[:, 0:n], AF.Relu, [pk], [("rl", rb)])
                    VE("tensor_tensor", [("rl", rb)], [("aT", wb, fc)], eng="pool", out=aT[:, wb, fc, n0_:n1_],
                       in0=rl[:, rb, 0:n], in1=rl[:, rb, 0:n], op=ALU.mult)
            ak = [("aT", wb, fc) for fc in range(8)]
            for j in range(ntile + (1 if half == 1 else 0)):
                if j < ntile:
                    t = half * 8 + j
                    rows, sl = 128, slice(j * 128, (j + 1) * 128)
                    xv, xk = xP[:, t, :], ("xP", t)
                else:
                    rows, sl = R, slice(1024, 1024 + R)
                    xv, xk = xS[:, :], "xS"
                for hf in range(2):
                    ps, pk = rot()
                    for fc in range(8):
                        MM(ps[0:rows, :], aT[:, wb, fc, sl], Wdn[:, wb, fc, hf * 512:(hf + 1) * 512], fc == 0, fc == 7,
                           ak + [("Wdn", wb)], [pk])
                    VE("tensor_tensor", [pk, xk], [xk], out=xv[0:rows, hf * 512:(hf + 1) * 512], in0=ps[0:rows, :],
                       in1=xv[0:rows, hf * 512:(hf + 1) * 512], op=ALU.add)


_NC_CACHE = {}


def kernel(**inputs):
    f32 = lambda a: np.ascontiguousarray(np.asarray(a, dtype=np.float32))
    inp = {k: f32(v) for k, v in inputs.items()}
    if "nc" not in _NC_CACHE:
        _NC_CACHE["nc"] = build_nc()[0]
    nc = _NC_CACHE["nc"]
    wnames = [n for n, _ in IN_SPECS[9:]]
    in_maps = []
    for cid in range(NCORES):
        sl = slice(cid * NS, (cid + 1) * NS)
        m = {
            "xp": inp["x_prompt"][cid],
            "xs": inp["x_sample"][sl, 0, :],
            "mem": inp["mem_prompt"][cid],
            "ck": inp["cache_mem_k"][:, sl].reshape(DEPTH, NS, MEM, D),
            "cv": inp["cache_mem_v"][:, sl].reshape(DEPTH, NS, MEM, D),
            "sC": inp["state_C"][:, sl],
            "sn": inp["state_n"][:, sl].reshape(DEPTH, NS, 512),
            "sm": inp["state_m"][:, sl],
            "scv": inp["state_conv"][:, sl],
        }
        for n in wnames:
            m[n] = inp[n]
        in_maps.append({k: np.ascontiguousarray(v) for k, v in m.items()})
    res = run_bass_kernel_spmd(nc, in_maps, core_ids=list(range(NCORES)))
    r = res.results
    cat = lambda name, axis: np.concatenate([np.asarray(r[i][name]) for i in range(NCORES)], axis=axis)
    stack = lambda name, axis: np.stack([np.asarray(r[i][name]) for i in range(NCORES)], axis=axis)
    y_prompt = stack("y_p", 0)
    y_sample = cat("y_s", 0).reshape(NCORES * NS, 1, D)
    mk = stack("mk_o", 1).reshape(DEPTH, NCORES, MEM, 4, 256)
    mv = stack("mv_o", 1).reshape(DEPTH, NCORES, MEM, 4, 256)
    C_p = stack("C_p", 1)
    n_p = stack("n_p", 1)
    m_p = stack("m_p", 1)
    cv_p = stack("cv_p", 1)
    C_s = cat("C_s", 1)
    n_s = cat("n_s", 1).reshape(DEPTH, NCORES * NS, 4, 128)
    m_s = cat("m_s", 1)
    cv_s = cat("cv_s", 1)
    gv_s = cat("gv_s", 1).reshape(DEPTH, NCORES * NS, 1, 512)
    outs = (y_prompt, y_sample, mk, mv, C_p, n_p, m_p, cv_p, C_s, n_s, m_s, cv_s, gv_s)
    return tuple(np.ascontiguousarray(o, dtype=np.float32) for o in outs)
```
